# Optimizing a Trainium2 kernel written in Bass

```python
import jax, jax.numpy as jnp
from jax import lax
import numpy as np

D_MODEL = 1024
BATCH = 2
SEQ = 8192
DEPTH = 1
DEC_BATCH = 8
DEC_SEQ = 32
PAST_LEN = 4096

CHUNK = 64
Q_BLOCK = 128
H_A = 8
NOPE = 64
ROPE = 32
DV = 64
Q_RANK = 384
KV_RANK = 256
ROPE_BASE = 10000.0
H_B = 8
D_HB = 64
D_FF = 2816
CONV_W = 3
EPS = 1e-6
NEG = -1e30
MLA_SCALE = (NOPE + ROPE) ** -0.5
SB_SCALE = D_HB ** -0.5
IN_SPLITS = [int(v) for v in np.cumsum([Q_RANK, KV_RANK, ROPE, H_B * D_HB, H_B * D_HB, H_B * D_HB, D_MODEL])]
N_IN = Q_RANK + KV_RANK + ROPE + 3 * H_B * D_HB + 2 * D_MODEL

kernel_name = "mla_stickbreaking_gated_hybrid_streaming_step"


def rms_norm(x, g):
    xf = x.astype(jnp.float32)
    y = xf * lax.rsqrt(jnp.mean(xf * xf, axis=-1, keepdims=True) + EPS)
    return (y * g.astype(jnp.float32)).astype(x.dtype)


def rope_tables(pos, dtype):
    inv = ROPE_BASE ** (-jnp.arange(0, ROPE, 2, dtype=jnp.float32) / ROPE)
    ang = pos.astype(jnp.float32)[:, None] * inv[None, :]
    return jnp.cos(ang).astype(dtype), jnp.sin(ang).astype(dtype)


def apply_rope(x, cos, sin):
    x1, x2 = x[..., :ROPE // 2], x[..., ROPE // 2:]
    return jnp.concatenate([x1 * cos - x2 * sin, x2 * cos + x1 * sin], axis=-1)


def mixer_inputs(h, pos, w_in, g_q_lat, w_uq, g_kv_lat):
    B, T, _ = h.shape
    p = h @ w_in
    q_lat, c_kv, k_r, q_b, k_b, v_b, g_a, g_b = jnp.split(p, IN_SPLITS, axis=-1)
    q = (rms_norm(q_lat, g_q_lat) @ w_uq).reshape(B, T, H_A, NOPE + ROPE)
    cos, sin = rope_tables(pos, h.dtype)
    q_nope = q[..., :NOPE]
    q_rope = apply_rope(q[..., NOPE:], cos[:, None, :], sin[:, None, :])
    k_rope = apply_rope(k_r, cos, sin)
    c_kv = rms_norm(c_kv, g_kv_lat)
    shp = (B, T, H_B, D_HB)
    return (q_nope, q_rope, c_kv, k_rope, q_b.reshape(shp), k_b.reshape(shp), v_b.reshape(shp),
            jax.nn.sigmoid(g_a), jax.nn.sigmoid(g_b))


def expand_latent(c_kv, w_uk, w_uv):
    B, S, _ = c_kv.shape
    return (c_kv @ w_uk).reshape(B, S, H_A, NOPE), (c_kv @ w_uv).reshape(B, S, H_A, DV)


def mla_attend(q_nope, q_rope, qpos, k_nope, k_rope, v, kpos):
    s = (jnp.einsum('bqhd,bshd->bhqs', q_nope, k_nope)
         + jnp.einsum('bqhr,bsr->bhqs', q_rope, k_rope)).astype(jnp.float32) * MLA_SCALE
    mask = (kpos[None, :] // CHUNK) <= (qpos[:, None] // CHUNK)
    p = jax.nn.softmax(jnp.where(mask, s, NEG), axis=-1)
    return jnp.einsum('bhqs,bshd->bqhd', p.astype(v.dtype), v)


def sb_attend(q, qpos, k, v, kpos):
    z = jnp.einsum('bqhd,bshd->bhqs', q, k).astype(jnp.float32) * SB_SCALE
    mask = kpos[None, :] < qpos[:, None]
    sp = jnp.where(mask, jax.nn.softplus(z), 0.0)
    later = lax.cumsum(sp, axis=3, reverse=True) - sp
    a = jnp.where(mask, jnp.exp(jax.nn.log_sigmoid(z) - later), 0.0)
    return jnp.einsum('bhqs,bshd->bqhd', a.astype(v.dtype), v)


def sweep_query_blocks(attend, qs, qpos, kvs, kpos):
    B, T = qs[0].shape[:2]
    nb = T // Q_BLOCK
    qs_b = tuple(jnp.moveaxis(q.reshape((B, nb, Q_BLOCK) + q.shape[2:]), 1, 0) for q in qs)
    qpos_b = qpos.reshape(nb, Q_BLOCK)

    def body(args):
        return attend(*args[:-1], args[-1], *kvs, kpos)

    out = lax.map(body, qs_b + (qpos_b,))
    return jnp.moveaxis(out, 0, 1).reshape((B, T) + out.shape[3:])


def conv_ffn(h, conv_state, w_up, conv_w, conv_b, w_down):
    T = h.shape[1]
    u = h @ w_up
    u_ext = jnp.concatenate([conv_state, u], axis=1)
    y = conv_b
    for i in range(CONV_W):
        y = y + conv_w[i] * u_ext[:, i:i + T]
    a, b = jnp.split(y, 2, axis=-1)
    return (jax.nn.gelu(a, approximate=True) * b) @ w_down, u_ext[:, T:]


def encoder_layer(x, c, pos, past, w_ada, b_ada, g_pre_mix, g_post_mix, g_pre_ffn, g_post_ffn,
                  w_in, g_q_lat, w_uq, g_kv_lat, w_uk, w_uv, w_proj_a, w_proj_b, w_out,
                  w_up, conv_w, conv_b, w_down):
    B, T, _ = x.shape
    ada = jax.nn.silu(c) @ w_ada + b_ada
    sh1, sc1, gt1, sh2, sc2, gt2 = [a[:, None, :] for a in jnp.split(ada, 6, axis=-1)]
    h = rms_norm(x, g_pre_mix) * (1 + sc1) + sh1
    q_nope, q_rope, c_kv, k_rope, q_b, k_b, v_b, g_a, g_b = mixer_inputs(h, pos, w_in, g_q_lat, w_uq, g_kv_lat)
    if past is None:
        k_nope, v_a = expand_latent(c_kv, w_uk, w_uv)
        o_a = sweep_query_blocks(mla_attend, (q_nope, q_rope), pos, (k_nope, k_rope, v_a), pos)
        o_b = sweep_query_blocks(sb_attend, (q_b,), pos, (k_b, v_b), pos)
        conv_state = jnp.zeros((B, CONV_W - 1, 2 * D_FF), x.dtype)
    else:
        past_ckv, past_krope, past_k, past_v, conv_state = past
        kpos = jnp.arange(past_ckv.shape[1] + T, dtype=jnp.int32)
        k_nope, v_a = expand_latent(jnp.concatenate([past_ckv, c_kv], axis=1), w_uk, w_uv)
        o_a = mla_attend(q_nope, q_rope, pos, k_nope, jnp.concatenate([past_krope, k_rope], axis=1), v_a, kpos)
        o_b = sb_attend(q_b, pos, jnp.concatenate([past_k, k_b], axis=1),
                        jnp.concatenate([past_v, v_b], axis=1), kpos)
    merged = (g_a * (o_a.reshape(B, T, H_A * DV) @ w_proj_a)
              + g_b * (o_b.reshape(B, T, H_B * D_HB) @ w_proj_b))
    x = x + gt1 * rms_norm(merged @ w_out, g_post_mix)
    h2 = rms_norm(x, g_pre_ffn) * (1 + sc2) + sh2
    f, new_conv = conv_ffn(h2, conv_state, w_up, conv_w, conv_b, w_down)
    x = x + gt2 * rms_norm(f, g_post_ffn)
    return x, (c_kv, k_rope, k_b, v_b, new_conv)


def setup_inputs(seed: int = 0) -> dict:
    key = jax.random.key(seed)
    ks = iter(jax.random.split(key, 40))
    nrm = lambda shape, s=1.0: jax.random.normal(next(ks), shape, jnp.float32) * s
    gain = lambda n: 1.0 + nrm((DEPTH, n), 0.05)
    L = DEPTH
    return {
        "x_prompt": nrm((BATCH, SEQ, D_MODEL)),
        "x_sample": nrm((DEC_BATCH, DEC_SEQ, D_MODEL)),
        "cache_mla_ckv": nrm((L, DEC_BATCH, PAST_LEN, KV_RANK)),
        "cache_mla_krope": nrm((L, DEC_BATCH, PAST_LEN, ROPE)),
        "cache_sb_k": nrm((L, DEC_BATCH, PAST_LEN, H_B, D_HB)),
        "cache_sb_v": nrm((L, DEC_BATCH, PAST_LEN, H_B, D_HB)),
        "state_ffn_conv": nrm((L, DEC_BATCH, CONV_W - 1, 2 * D_FF)),
        "c_prompt": nrm((BATCH, D_MODEL)),
        "c_sample": nrm((DEC_BATCH, D_MODEL)),
        "w_ada": nrm((L, D_MODEL, 6 * D_MODEL), D_MODEL ** -0.5),
        "b_ada": nrm((L, 6 * D_MODEL), 0.01),
        "g_pre_mix": gain(D_MODEL),
        "g_post_mix": gain(D_MODEL),
        "g_pre_ffn": gain(D_MODEL),
        "g_post_ffn": gain(D_MODEL),
        "w_in": nrm((L, D_MODEL, N_IN), D_MODEL ** -0.5),
        "g_q_lat": gain(Q_RANK),
        "w_uq": nrm((L, Q_RANK, H_A * (NOPE + ROPE)), Q_RANK ** -0.5),
        "g_kv_lat": gain(KV_RANK),
        "w_uk": nrm((L, KV_RANK, H_A * NOPE), KV_RANK ** -0.5),
        "w_uv": nrm((L, KV_RANK, H_A * DV), KV_RANK ** -0.5),
        "w_proj_a": nrm((L, H_A * DV, D_MODEL), (H_A * DV) ** -0.5),
        "w_proj_b": nrm((L, H_B * D_HB, D_MODEL), (H_B * D_HB) ** -0.5),
        "w_out": nrm((L, D_MODEL, D_MODEL), D_MODEL ** -0.5),
        "w_up": nrm((L, D_MODEL, 2 * D_FF), D_MODEL ** -0.5),
        "conv_w": nrm((L, CONV_W, 2 * D_FF), CONV_W ** -0.5),
        "conv_b": nrm((L, 2 * D_FF), 0.01),
        "w_down": nrm((L, D_FF, D_MODEL), D_FF ** -0.5),
    }


def reference(x_prompt, x_sample, cache_mla_ckv, cache_mla_krope, cache_sb_k, cache_sb_v, state_ffn_conv,
              c_prompt, c_sample, w_ada, b_ada, g_pre_mix, g_post_mix, g_pre_ffn, g_post_ffn,
              w_in, g_q_lat, w_uq, g_kv_lat, w_uk, w_uv, w_proj_a, w_proj_b, w_out,
              w_up, conv_w, conv_b, w_down):
    past_len = cache_mla_ckv.shape[2]
    pos_p = jnp.arange(x_prompt.shape[1], dtype=jnp.int32)
    pos_s = past_len + jnp.arange(x_sample.shape[1], dtype=jnp.int32)
    xp, xs = x_prompt, x_sample
    st_p = [[] for _ in range(5)]
    st_s = [[] for _ in range(5)]
    for l in range(DEPTH):
        w = (w_ada[l], b_ada[l], g_pre_mix[l], g_post_mix[l], g_pre_ffn[l], g_post_ffn[l],
             w_in[l], g_q_lat[l], w_uq[l], g_kv_lat[l], w_uk[l], w_uv[l], w_proj_a[l], w_proj_b[l], w_out[l],
             w_up[l], conv_w[l], conv_b[l], w_down[l])
        xp, sp = encoder_layer(xp, c_prompt, pos_p, None, *w)
        past = (cache_mla_ckv[l], cache_mla_krope[l], cache_sb_k[l], cache_sb_v[l], state_ffn_conv[l])
        xs, ss = encoder_layer(xs, c_sample, pos_s, past, *w)
        for i in range(5):
            st_p[i].append(sp[i])
            st_s[i].append(ss[i])
    p_ckv, p_kr, p_k, p_v, p_conv = [jnp.stack(a, axis=0) for a in st_p]
    s_ckv, s_kr, s_k, s_v, s_conv = [jnp.stack(a, axis=0) for a in st_s]
    return (xp, xs, p_ckv, p_kr, p_k, p_v, p_conv, s_ckv, s_kr, s_k, s_v, s_conv)
```

```python
import numpy as np
import concourse.bass as bass
import concourse.mybir as mybir
from concourse.bass_utils import run_bass_kernel_spmd

F32 = mybir.dt.float32
BF16 = mybir.dt.bfloat16
F32R = mybir.dt.float32r
I32 = mybir.dt.int32
AF = mybir.ActivationFunctionType
ALU = mybir.AluOpType

ENGS = ("pe", "act", "dve", "pool", "sp")
DPOOL = {"sp": 20, "pool": 8, "act": 6}


class Res:
    __slots__ = ("w", "r", "name", "x")

    def __init__(self, name="", x=False):
        self.w = None
        self.r = []
        self.name = name
        self.x = x


class Prog:
    def __init__(self):
        self.q = {e: [] for e in ENGS}
        self.cnt = {e: 0 for e in ENGS}
        self.seen = {e: {} for e in ENGS}
        self.dma_cnt = {}
        self.dma_gen = {}

    def _deps(self, eng, reads, writes):
        toks = []
        for r in reads:
            if r.w is not None:
                toks.append(r.w)
            if r.x:
                toks.extend(t for t in r.r if t[0] != eng)
        for w in writes:
            if w.w is not None:
                toks.append(w.w)
            toks.extend(w.r)
        waits = {}
        seen = self.seen[eng]
        for (k, v) in toks:
            if k == eng and eng == "pe":
                continue
            if v > seen.get(k, 0) and v > waits.get(k, 0):
                waits[k] = v
        for k, v in waits.items():
            seen[k] = v
        return waits

    def _mark(self, tok, reads, writes):
        for r in reads:
            r.r.append(tok)
        for w in writes:
            w.w = tok
            w.r = []

    def op(self, eng, fn, reads=(), writes=(), inc=True):
        waits = self._deps(eng, reads, writes)
        if inc:
            self.cnt[eng] += 1
            tok = (eng, self.cnt[eng])
        else:
            tok = (eng, self.cnt[eng] + 1)
        self.q[eng].append((waits, fn, (eng, 1) if inc else None))
        self._mark(tok, reads, writes)
        return tok

    def dma(self, eng, fn, reads=(), writes=()):
        npool = DPOOL[eng]
        idx = self.dma_cnt.get(eng, 0)
        self.dma_cnt[eng] = idx + 1
        key = ("d", eng, idx % npool)
        waits = self._deps(eng, reads, writes)
        g = self.dma_gen.get(key, 0)
        if g > 0:
            if self.seen[eng].get(key, 0) < 16 * g:
                waits[key] = 16 * g
                self.seen[eng][key] = 16 * g
        self.dma_gen[key] = g + 1
        tok = (key, 16 * (g + 1))
        self.q[eng].append((waits, fn, (key, 16)))
        self._mark(tok, reads, writes)
        return tok

    def barrier(self, allres=()):
        toks = [(e, self.cnt[e]) for e in ENGS if self.cnt[e] > 0]
        toks += [(k, 16 * g) for k, g in self.dma_gen.items() if g > 0]
        for e in ENGS:
            waits = {}
            for (k, v) in toks:
                if k == e:
                    continue
                if v > self.seen[e].get(k, 0):
                    waits[k] = v
                    self.seen[e][k] = v
            if waits:
                self.q[e].append((waits, None, None))

    def make_sems(self, nc, stack):
        sems = {}
        for e in ENGS:
            sems[e] = stack.enter_context(nc.semaphore("s_" + e))
        for e, n in DPOOL.items():
            for k in range(n):
                sems[("d", e, k)] = stack.enter_context(nc.semaphore("s_d%s%d" % (e, k)))
        self.sems = sems

    def emit(self, nc, stack):
        sems = self.sems
        self.barrier()
        block = stack.enter_context(nc.Block())
        q = self.q
        self.q = {e: [] for e in ENGS}

        def run(engobj, name):
            for (waits, fn, inc) in q[name]:
                for k, v in waits.items():
                    engobj.wait_ge(sems[k], v)
                if fn is None:
                    continue
                ins = fn(engobj)
                if inc is not None:
                    ins.then_inc(sems[inc[0]], inc[1])

        @block.sync
        def _(e):
            run(e, "sp")

        @block.tensor
        def _(e):
            run(e, "pe")

        @block.scalar
        def _(e):
            run(e, "act")

        @block.vector
        def _(e):
            run(e, "dve")

        @block.gpsimd
        def _(e):
            run(e, "pool")


from contextlib import ExitStack
import ml_dtypes

D = 1024
NCH = 8
QR = 384
KVR = 256
ROPE = 32
HB = 512
DFF = 2816
NFC = 22
EPS = 1e-6
NBUF = 8192
OWN0 = 6144
RUN_ROW0 = (3072, 7168)
RUN_COL0 = (0, 1026)
ROWS = 2048
NOWN = ROWS + 4


def own_bufrow(col):
    r = 0 if col < RUN_COL0[1] else 1
    return RUN_ROW0[r] - 2 + (col - RUN_COL0[r])


def own_outrow(col):
    return col - 2 if col < RUN_COL0[1] else col - 4

DEC = 32
PAST = 4096
SLEN = 4224
MLA_SCALE = float((64 + 32) ** -0.5)
NEGBIG = -30000.0
SB5 = True
C_QL, C_CKV, C_KR, C_SQ, C_SK, C_SV, C_GA, C_GB = 0, 384, 640, 672, 1184, 1696, 2208, 3232


class Ctx:
    _n = 0

    def __init__(self, nc, prog):
        self.nc = nc
        Ctx._n += 1
        self.pfx = "p%d_" % Ctx._n
        self.st = ExitStack()
        self.P = prog
        self.R = {}
        self.psn = set()
        self.store_q = "sp"

    def sb(self, name, shape, dt=F32):
        return self.st.enter_context(self.nc.sbuf_tensor(self.pfx + name, list(shape), dt))

    def ps(self, name, shape, dt=F32):
        self.psn.add(name)
        return self.st.enter_context(self.nc.psum_tensor(self.pfx + name, list(shape), dt))

    def r(self, names):
        out = []
        for n in names:
            if n not in self.R:
                self.R[n] = Res(n, n in self.psn)
            out.append(self.R[n])
        return out

    def mm(self, out, lhsT, rhs, start, stop, rd, wr, inc=True):
        self.P.op("pe", lambda e: e.matmul(out, lhsT=lhsT, rhs=rhs, start=start, stop=stop, skip_group_check=True),
                  reads=self.r(rd), writes=self.r(wr), inc=inc)

    def tr(self, out, in_, ident, rd, wr, inc=True):
        self.P.op("pe", lambda e: e.transpose(out=out, in_=in_, identity=ident), reads=self.r(rd), writes=self.r(wr), inc=inc)

    def act(self, out, in_, func, rd, wr, scale=None, bias=None, accum=None):
        kw = {}
        if scale is not None:
            kw["scale"] = scale
        if bias is not None:
            kw["bias"] = bias
        if accum is not None:
            kw["accum_out"] = accum
        self.P.op("act", lambda e: e.activation(out=out, in_=in_, func=func, **kw), reads=self.r(rd), writes=self.r(wr))

    def tt(self, out, in0, in1, op, rd, wr, eng="dve"):
        self.P.op(eng, lambda e: e.tensor_tensor(out=out, in0=in0, in1=in1, op=op), reads=self.r(rd), writes=self.r(wr))

    def ts(self, out, in0, s1, s2, op0, op1, rd, wr, eng="dve"):
        if op1 is None:
            self.P.op(eng, lambda e: e.tensor_scalar(out=out, in0=in0, scalar1=s1, scalar2=None, op0=op0), reads=self.r(rd), writes=self.r(wr))
        else:
            self.P.op(eng, lambda e: e.tensor_scalar(out=out, in0=in0, scalar1=s1, scalar2=s2, op0=op0, op1=op1), reads=self.r(rd), writes=self.r(wr))

    def stt(self, out, in0, scalar, in1, op0, op1, rd, wr):
        self.P.op("dve", lambda e: e.scalar_tensor_tensor(out=out, in0=in0, scalar=scalar, in1=in1, op0=op0, op1=op1),
                  reads=self.r(rd), writes=self.r(wr))

    def cp(self, out, in_, rd, wr, eng="dve"):
        if eng == "act":
            self.act(out, in_, AF.Identity, rd, wr)
        else:
            self.P.op(eng, lambda e: e.tensor_copy(out=out, in_=in_), reads=self.r(rd), writes=self.r(wr))

    def memset(self, ap, val, wr, eng="dve"):
        self.P.op(eng, lambda e: e.memset(ap, val), writes=self.r(wr))

    def recip(self, out, in_, rd, wr):
        self.P.op("dve", lambda e: e.reciprocal(out=out, in_=in_), reads=self.r(rd), writes=self.r(wr))

    def ld(self, out, in_, wr, q="sp", rd=(), slow=False):
        if slow:
            self.P.dma(q, lambda e: e.dma_start(out=out, in_=in_, allow_slow_non_contiguous=True), reads=self.r(rd), writes=self.r(wr))
        else:
            self.P.dma(q, lambda e: e.dma_start(out=out, in_=in_), reads=self.r(rd), writes=self.r(wr))

    def store(self, out, in_, rd, q=None, slow=False):
        q = q or self.store_q
        if slow:
            self.P.dma(q, lambda e: e.dma_start(out=out, in_=in_, allow_slow_non_contiguous=True), reads=self.r(rd))
        else:
            self.P.dma(q, lambda e: e.dma_start(out=out, in_=in_), reads=self.r(rd))

    def rstd(self, stat, n, col, width, rname):
        self.act(stat[0:n, col + 1:col + 2], stat[0:n, col:col + 1], AF.Ln, [rname], [rname], scale=1.0 / width, bias=EPS)
        self.act(stat[0:n, col + 1:col + 2], stat[0:n, col + 1:col + 2], AF.Exp, [rname], [rname], scale=-0.5)

    def finish(self):
        self.P.emit(self.nc, self.st)
        self.st.close()


def fm(vec, p=128):
    return vec.rearrange("(c p) -> p c", p=p)


def qtiles(job):
    if job == 0:
        out = []
        for r in range(2):
            ch0 = RUN_ROW0[r] // 512
            out.append((RUN_COL0[r], 2, ch0, "halo"))
            for k in range(2):
                out.append((RUN_COL0[r] + 2 + 512 * k, 512, ch0 + k + 1, "big"))
        return out
    return [(0, DEC, 9, "samp")]


def build_nc(debug=False, nphase=4, alim=9, ntl=None):
    nc = bass.Bass("TRN2", target_bir_lowering=False)
    din = lambda n, s, d=F32: nc.dram_tensor(n, list(s), d, kind="ExternalInput").ap()
    dout = lambda n, s: nc.dram_tensor(n, list(s), F32, kind="ExternalOutput").ap()
    dscr = lambda n, s, d=BF16: nc.dram_tensor(n, list(s), d, kind=("ExternalOutput" if debug else "Internal")).ap()

    xk = din("xk", [NBUF, D]); x_s = din("x_s", [DEC, D]); c2 = din("c2", [2, D])
    w_ada = din("w_ada", [D, 6 * D]); b_ada = din("b_ada", [6 * D])
    g_pre = din("g_pre", [D]); g_post = din("g_post", [D]); g_pre2 = din("g_pre2", [D]); g_post2 = din("g_post2", [D])
    g_kv = din("g_kv", [1, KVR]); g_q = din("g_q", [QR])
    w_in = din("w_in", [D, 4256]); w_uq = din("w_uq", [QR, 768]); w_uk = din("w_uk", [KVR, 512]); w_uv = din("w_uv", [KVR, 512])
    w_pa = din("w_pa", [512, D]); w_pb = din("w_pb", [512, D]); w_out = din("w_out", [D, D])
    w_up = din("w_up", [D, 2 * DFF]); conv_w = din("conv_w", [3, 2 * DFF]); conv_b = din("conv_b", [2 * DFF]); w_dn = din("w_dn", [DFF, D])
    cs_p = din("cs_p", [NBUF, 32]); cs_s = din("cs_s", [DEC, 32])
    cq = [din("cq_p", [32, NOWN]), din("cq_s", [32, DEC])]
    sq = [din("sq_p", [32, NOWN]), din("sq_s", [32, DEC])]
    kbias = [din("kb_p", [1, NBUF], BF16), din("kb_s", [1, SLEN], BF16)]
    ident = din("ident", [128, 128], BF16); mtri_d = din("mtri", [128, 128], BF16); negu_d = din("negu", [128, 128], BF16)
    ones_d = din("ones", [1, 4096], BF16); hv_d = din("hv", [1, 2])
    ca_ckv = din("ca_ckv", [PAST, KVR]); ca_kr = din("ca_kr", [PAST, ROPE]); ca_k = din("ca_k", [PAST, HB]); ca_v = din("ca_v", [PAST, HB])
    st_conv = din("st_conv", [2, 2 * DFF])
    y = [dout("y_p", [ROWS, D]), dout("y_s", [DEC, D])]
    o_ckv = [dout("o_ckv_p", [ROWS, KVR]), dout("o_ckv_s", [DEC, KVR])]
    o_kr = [dout("o_kr_p", [ROWS, ROPE]), dout("o_kr_s", [DEC, ROPE])]
    o_k = [dout("o_k_p", [ROWS, HB]), dout("o_k_s", [DEC, HB])]
    o_v = [dout("o_v_p", [ROWS, HB]), dout("o_v_s", [DEC, HB])]
    o_conv = [dout("o_conv_p", [2, 2 * DFF]), dout("o_conv_s", [2, 2 * DFF])]
    SL = [NBUF, SLEN]
    NO = [NOWN, DEC]
    KN = [dscr("KN%d" % j, [8, 64, SL[j]]) for j in range(2)]
    KRT = [dscr("KRT%d" % j, [32, SL[j]]) for j in range(2)]
    VA = [dscr("VA%d" % j, [SL[j], 8 * 128]) for j in range(2)]
    SKT = [dscr("SKT%d" % j, [8, 64, SL[j]]) for j in range(2)]
    SV = [dscr("SV%d" % j, [SL[j], HB]) for j in range(2)]
    HT = [dscr("HT%d" % j, [NCH, 128, NO[j]]) for j in range(2)]
    QLT = [dscr("QLT%d" % j, [3, 128, NO[j]]) for j in range(2)]
    OT = [dscr("OT%d" % j, [16, 64, NO[j]]) for j in range(2)]
    X1 = [dscr("X1%d" % j, [NO[j], D], F32) for j in range(2)]
    MOD = dscr("MOD", [2, 6, D], F32)
    WUPB = dscr("WUPB", [NFC, 128, NCH, 2, 128])

    PROG = Prog()
    semstack = ExitStack()
    PROG.make_sems(nc, semstack)
    C = Ctx(nc, PROG)
    wada = [C.sb("wada%d" % i, [128, NCH, 512]) for i in range(3)]
    wkv = C.sb("wkv", [128, NCH, 1312 + QR], BF16)
    wuk = C.sb("wuk", [128, 2, 512], BF16); wuv = C.sb("wuv", [128, 2, 512], BF16)
    idb = C.sb("idb", [128, 128], BF16)
    cT = C.sb("cT", [128, NCH, 2]); sT = C.sb("sT", [128, NCH, 2])
    bada = C.sb("bada", [128, 48]); adaT = C.sb("adaT", [128, 48, 2])
    gv = C.sb("gv", [128, 4, NCH])
    gq = C.sb("gq", [128, 3])
    modt = C.sb("modt", [128, 2, 6, NCH])
    gkv = C.sb("gkv", [128, KVR])
    pk = C.ps("pk", [128, 512]); pv = C.ps("pv", [128, 512]); pc = C.ps("pc", [128, 512])
    ada_ps = pc
    C.store_q = "pool"
    for t in range(2):
        C.ld(cT[:, :, t], fm(c2[t]), ["cT"], slow=True)
    C.ld(bada[:, :], fm(b_ada), ["bada"], slow=True)
    for i, g in enumerate((g_pre, g_post, g_pre2, g_post2)):
        C.ld(gv[:, i, :], fm(g), ["gv"], slow=True)
    C.ld(gq[:, :], fm(g_q), ["gq"], slow=True)
    C.ld(idb[:, :], ident[:, :], ["idb"])
    C.ld(gkv[:, :], g_kv[0:1, :].partition_broadcast(128), ["gkv"])
    def load_wkv():
        for c in range(NCH):
            rows = slice(c * 128, (c + 1) * 128)
            C.ld(wkv[:, c, 0:288], w_in[rows, C_CKV:C_CKV + 288], ["wkv"], q="pool")
            C.ld(wkv[:, c, 288:1312], w_in[rows, C_SK:C_SK + 1024], ["wkv"], q="pool")
            C.ld(wkv[:, c, 1312:1312 + QR], w_in[rows, C_QL:C_QL + QR], ["wkv"], q="pool")

    for c in range(2):
        C.ld(wuk[:, c, :], w_uk[c * 128:(c + 1) * 128, :], ["wuk"], q="pool")
        C.ld(wuv[:, c, :], w_uv[c * 128:(c + 1) * 128, :], ["wuv"], q="pool")
    def ada_group(g):
        if g == 0:
            C.act(sT[:, :, :], cT[:, :, :], AF.Silu, ["cT"], ["sT"])
        for gg in ([0, 1, 2] if g == 0 else [g + 2]):
            if gg < 12:
                C.ld(wada[gg % 3][:, :, :], w_ada[:, gg * 512:(gg + 1) * 512].rearrange("(c p) n -> p c n", p=128), ["wada%d" % (gg % 3)])
        if True:
            wb = wada[g % 3]; wn = "wada%d" % (g % 3)
            for q4 in range(4):
                for c in range(NCH):
                    C.mm(ada_ps[:, q4 * 2:q4 * 2 + 2], wb[:, c, q4 * 128:(q4 + 1) * 128], sT[:, c, :], c == 0, c == NCH - 1,
                         [wn, "sT"], ["pc"], inc=(c == NCH - 1))
            for t in range(2):
                C.tt(adaT[:, g * 4:(g + 1) * 4, t], ada_ps[:, 0:8].rearrange("p (q t) -> p q t", t=2)[:, :, t], bada[:, g * 4:(g + 1) * 4],
                     ALU.add, ["pc", "bada"], ["adaT"])

    def ada_final():
        for t in range(2):
            C.stt(modt[:, t, 0, :], adaT[:, 8:16, t], 1.0, gv[:, 0, :], ALU.add, ALU.mult, ["adaT", "gv"], ["modt"])
            C.cp(modt[:, t, 1, :], adaT[:, 0:8, t], ["adaT"], ["modt"])
            C.stt(modt[:, t, 2, :], adaT[:, 32:40, t], 1.0, gv[:, 2, :], ALU.add, ALU.mult, ["adaT", "gv"], ["modt"])
            C.cp(modt[:, t, 3, :], adaT[:, 24:32, t], ["adaT"], ["modt"])
            C.tt(modt[:, t, 4, :], adaT[:, 16:24, t], gv[:, 1, :], ALU.mult, ["adaT", "gv"], ["modt"])
            C.tt(modt[:, t, 5, :], adaT[:, 40:48, t], gv[:, 3, :], ALU.mult, ["adaT", "gv"], ["modt"])
            for k in range(6):
                C.store(fm(MOD[t, k]), modt[:, t, k, :], ["modt"], slow=True)


    xt = [C.sb("xt%d" % i, [128, D]) for i in range(2)]
    xn = [C.sb("xn%d" % i, [128, D], BF16) for i in range(2)]
    htmp = C.sb("htmp", [128, NCH, 128])
    junk = C.sb("junk", [128, D], BF16)
    hT = [C.sb("hT%d" % i, [128, NCH, 128], BF16) for i in range(4)]
    cs = [C.sb("cs%d" % i, [128, 32]) for i in range(3)]
    stA = [C.sb("stA%d" % i, [128, 2]) for i in range(2)]
    stB = [C.sb("stB%d" % i, [128, 2]) for i in range(2)]
    stC = [C.sb("stC%d" % i, [128, 2]) for i in range(2)]
    okv = [C.sb("okv%d" % i, [128, 2 * HB]) for i in range(2)]
    ock = [C.sb("ock%d" % i, [128, KVR + ROPE]) for i in range(2)]
    rt = [C.sb("rt%d" % i, [128, 64]) for i in range(2)]
    CB = [C.sb("CB%d" % i, [128, KVR + ROPE], BF16) for i in range(4)]
    KB = [C.sb("KB%d" % i, [128, HB], BF16) for i in range(4)]
    SVB = [C.sb("SVB%d" % i, [128, HB], BF16) for i in range(4)]
    CT = [C.sb("CT%d" % i, [128, 3, 128], BF16) for i in range(2)]
    STt = [C.sb("ST%d" % i, [128, 4, 128], BF16) for i in range(2)]
    KNT = [C.sb("KNT%d" % i, [64, 8, 128], BF16) for i in range(2)]
    VAT = [C.sb("VAT%d" % i, [128, 8, 128], BF16) for i in range(2)]
    QN = [C.sb("QN%d" % i, [128, QR], BF16) for i in range(2)]
    QT = [C.sb("QT%d" % i, [128, 3, 128], BF16) for i in range(2)]
    tp = C.ps("tp", [128, NCH, 128], BF16)
    tpx = C.ps("tpx", [128, 8, 128], BF16)
    tq = C.ps("tq", [128, 8, 128], BF16)
    pkn = C.ps("pkn", [128, 4, 128])
    pva = C.ps("pva", [128, 4, 128])
    for i in range(2):
        C.memset(VAT[i][:, :, 64:128], 1.0, ["VAT%d" % i])

    def rstd2(stat, n, width, rname):
        C.act(stat[0:n, 1:2], stat[0:n, 0:1], AF.Ln, [rname], [rname], scale=1.0 / width, bias=EPS)
        C.act(stat[0:n, 1:2], stat[0:n, 1:2], AF.Exp, [rname], [rname], scale=-0.5)

    def kv_tail(i, job, r0, n, si=None):
        s = "%d" % i
        si = i if si is None else si
        ss = "%d" % si
        for c, (a, b) in enumerate(((0, 128), (128, 256), (256, 288))):
            C.tr(tpx[0:b - a, c, 0:n], CB[si][0:n, a:b], idb[0:n, 0:n], ["CB" + ss, "idb"], ["tpx"], inc=(c == 2))
        for c in range(4):
            C.tr(tpx[:, 3 + c, 0:n], KB[si][0:n, c * 128:(c + 1) * 128], idb[0:n, 0:n], ["KB" + ss, "idb"], ["tpx"], inc=(c == 3))
        C.cp(CT[i][:, 0:2, 0:n], tpx[:, 0:2, 0:n], ["tpx"], ["CT" + s])
        C.cp(CT[i][0:32, 2, 0:n], tpx[0:32, 2, 0:n], ["tpx"], ["CT" + s])
        C.cp(STt[i][:, :, 0:n], tpx[:, 3:7, 0:n], ["tpx"], ["ST" + s])
        C.store(SKT[job].rearrange("(pr two) d r -> (two d) pr r", two=2)[:, :, r0:r0 + n], STt[i][:, :, 0:n], ["ST" + s])
        C.store(KRT[job][:, r0:r0 + n], CT[i][0:32, 2, 0:n], ["CT" + s])
        for (bk, bn, h0) in ((pkn, "pkn", 0), (pva, "pva", 4)):
            for h in range(4):
                for c in range(2):
                    C.mm(bk[0:64, h, 0:n], wuk[:, c, (h0 + h) * 64:(h0 + h + 1) * 64], CT[i][:, c, 0:n], c == 0, c == 1,
                         ["wuk", "CT" + s], [bn], inc=(c == 1 and h == 3))
        C.cp(KNT[i][:, 0:4, 0:n], pkn[0:64, :, 0:n], ["pkn"], ["KNT%s_a" % s], eng="act")
        C.cp(KNT[i][:, 4:8, 0:n], pva[0:64, :, 0:n], ["pva"], ["KNT%s_b" % s])
        C.store(KN[job].rearrange("h d r -> d h r")[:, :, r0:r0 + n], KNT[i][:, :, 0:n], ["KNT%s_a" % s, "KNT%s_b" % s])
        for c in range(2):
            C.mm(pva[0:n, :, :], CT[i][:, c, 0:n], wuv[:, c, :], c == 0, c == 1, ["wuv", "CT" + s], ["pva"], inc=(c == 1))
        C.cp(VAT[i][0:n, :, 0:64], pva[0:n, :, :].rearrange("p a (b e) -> p (a b) e", e=64), ["pva"], ["VAT" + s])
        C.store(VA[job][r0:r0 + n, :], VAT[i][0:n].rearrange("p h e -> p (h e)"), ["VAT" + s])

    tiles = []
    for t in range(NBUF // 128):
        own = None
        for r in range(2):
            t0 = RUN_ROW0[r] // 128
            if t == t0 - 1:
                own = (126, 128, RUN_COL0[r], None)
            elif t0 <= t < t0 + 8:
                own = (0, 128, RUN_COL0[r] + 2 + (t - t0) * 128, r * 1024 + (t - t0) * 128)
        tiles.append((0, xk, t * 128, 128, cs_p, own, t * 128))
    tiles.append((1, x_s, 0, DEC, cs_s, (0, DEC, 0, 0), PAST))
    if ntl is not None:
        tiles = tiles[-ntl:]

    def S0(ti):
        (job, xsrc, r0, n, cssrc, own, kr0) = tiles[ti]
        i2 = "%d" % (ti % 2); i3 = "%d" % (ti % 3); i4 = "%d" % (ti % 4)
        x_ = xt[ti % 2]; xn_ = xn[ti % 2]; st_ = stA[ti % 2]; h_ = hT[ti % 4]
        C.ld(x_[0:n, :], xsrc[r0:r0 + n, :], ["xt" + i2])
        C.ld(cs[ti % 3][0:n, :], cssrc[r0:r0 + n, :], ["cs" + i3])
        C.act(junk[0:n, :], x_[0:n, :], AF.Square, ["xt" + i2], ["junk", "stA" + i2], accum=st_[0:n, 0:1])
        rstd2(st_, n, D, "stA" + i2)
        C.ts(xn_[0:n, :], x_[0:n, :], st_[0:n, 1:2], None, ALU.mult, None, ["xt" + i2, "stA" + i2], ["xn" + i2])
        for c in range(NCH):
            C.tr(tp[:, c, 0:n], xn_[0:n, c * 128:(c + 1) * 128], idb[0:n, 0:n], ["xn" + i2, "idb"], ["tp"], inc=(c == NCH - 1))
        C.tt(htmp[:, :, 0:n], tp[:, :, 0:n], modt[:, job, 0, :].unsqueeze(2).broadcast_to([128, NCH, n]), ALU.mult, ["tp", "modt"], ["htmp"])
        C.tt(h_[:, :, 0:n], htmp[:, :, 0:n], modt[:, job, 1, :].unsqueeze(2).broadcast_to([128, NCH, n]), ALU.add, ["htmp", "modt"], ["hT" + i4])

    def S1(ti):
        (job, xsrc, r0, n, cssrc, own, kr0) = tiles[ti]
        i = ti % 2
        s = "%d" % i; i3 = "%d" % (ti % 3); i4 = "%d" % (ti % 4)
        h_ = hT[ti % 4]; cs_ = cs[ti % 3]; st_ = stB[i]
        for (bank, bn, c0, w) in ((pc, "pc", 0, 288), (pk, "pk", 288, HB), (pv, "pv", 800, HB)):
            for c in range(NCH):
                C.mm(bank[0:n, 0:w], h_[:, c, 0:n], wkv[:, c, c0:c0 + w], c == 0, c == NCH - 1, ["hT" + i4, "wkv"], [bn], inc=(c == NCH - 1))
        outs = own is not None and own[3] is not None
        C.cp(KB[i][0:n, :], pk[0:n, :], ["pk"], ["KB" + s], eng="act")
        C.cp(SVB[i][0:n, :], pv[0:n, :], ["pv"], ["SVB" + s])
        C.store(SV[job][kr0:kr0 + n, :], SVB[i][0:n, :], ["SVB" + s])
        if outs:
            C.cp(okv[i][0:n, 0:HB], pk[0:n, :], ["pk"], ["okv%s_a" % s], eng="act")
            C.cp(okv[i][0:n, HB:2 * HB], pv[0:n, :], ["pv"], ["okv%s_b" % s])
            C.store(o_k[job][own[3]:own[3] + n, :], okv[i][0:n, 0:HB], ["okv%s_a" % s])
            C.store(o_v[job][own[3]:own[3] + n, :], okv[i][0:n, HB:2 * HB], ["okv%s_b" % s])
        C.act(junk[0:n, 0:KVR], pc[0:n, 0:KVR], AF.Square, ["pc"], ["junk", "stB" + s], accum=st_[0:n, 0:1])
        rstd2(st_, n, KVR, "stB" + s)
        C.stt(ock[i][0:n, 0:KVR], pc[0:n, 0:KVR], st_[0:n, 1:2], gkv[0:n, :], ALU.mult, ALU.mult, ["pc", "stB" + s, "gkv"], ["ock" + s])
        x1 = pc[0:n, KVR:KVR + 16]; x2 = pc[0:n, KVR + 16:KVR + 32]
        cosv = cs_[0:n, 0:16]; sinv = cs_[0:n, 16:32]
        C.tt(rt[i][0:n, 0:16], x1, cosv, ALU.mult, ["pc", "cs" + i3], ["rt" + s])
        C.tt(rt[i][0:n, 16:32], x2, sinv, ALU.mult, ["pc", "cs" + i3], ["rt" + s])
        C.tt(rt[i][0:n, 32:48], x2, cosv, ALU.mult, ["pc", "cs" + i3], ["rt" + s])
        C.tt(rt[i][0:n, 48:64], x1, sinv, ALU.mult, ["pc", "cs" + i3], ["rt" + s])
        C.tt(ock[i][0:n, KVR:KVR + 16], rt[i][0:n, 0:16], rt[i][0:n, 16:32], ALU.subtract, ["rt" + s], ["ock" + s])
        C.tt(ock[i][0:n, KVR + 16:KVR + 32], rt[i][0:n, 32:48], rt[i][0:n, 48:64], ALU.add, ["rt" + s], ["ock" + s])
        C.cp(CB[i][0:n, :], ock[i][0:n, :], ["ock" + s], ["CB" + s], eng="pool")
        if outs:
            C.store(o_ckv[job][own[3]:own[3] + n, :], ock[i][0:n, 0:KVR], ["ock" + s])
            C.store(o_kr[job][own[3]:own[3] + n, :], ock[i][0:n, KVR:KVR + ROPE], ["ock" + s])

    def S2(ti):
        (job, xsrc, r0, n, cssrc, own, kr0) = tiles[ti]
        i = ti % 2
        s = "%d" % i; i4 = "%d" % (ti % 4)
        h_ = hT[ti % 4]; st_ = stC[i]
        if alim >= 3:
            kv_tail(i, job, kr0, n)
        if own is not None and alim >= 4:
            lo, hi, col0, _ = own
            w = hi - lo
            for c in range(NCH):
                C.mm(pkn[0:n, 0:3, :], h_[:, c, 0:n], wkv[:, c, 1312:1312 + QR], c == 0, c == NCH - 1, ["hT" + i4, "wkv"], ["pkn"], inc=(c == NCH - 1))
            C.act(junk[0:n, 0:QR], pkn[0:n, 0:3, :], AF.Square, ["pkn"], ["junk", "stC" + s], accum=st_[0:n, 0:1])
            rstd2(st_, n, QR, "stC" + s)
            C.ts(QN[i][0:n, :], pkn[0:n, 0:3, :], st_[0:n, 1:2], None, ALU.mult, None, ["pkn", "stC" + s], ["QN" + s])
            for c in range(3):
                C.tr(tq[:, c, 0:n], QN[i][0:n, c * 128:(c + 1) * 128], idb[0:n, 0:n], ["QN" + s, "idb"], ["tq"], inc=(c == 2))
            for c in range(3):
                C.ts(QT[i][:, c, 0:n], tq[:, c, 0:n], gq[:, c:c + 1], None, ALU.mult, None, ["tq", "gq"], ["QT" + s])
            C.store(QLT[job].rearrange("c p n -> p c n")[:, :, col0:col0 + w], QT[i][:, :, lo:hi], ["QT" + s])
            C.store(HT[job].rearrange("c p n -> p c n")[:, :, col0:col0 + w], h_[:, :, lo:hi], ["hT" + i4])

    C.store_q = "sp"
    ada_done = set()
    NCT = PAST // 128 if alim >= 5 else 0

    def cache_loads(t):
        si = t % 4
        ss = "%d" % si
        r0 = t * 128
        C.ld(CB[si][:, 0:KVR], ca_ckv[r0:r0 + 128, :], ["CB" + ss], q="pool")
        C.ld(CB[si][:, KVR:KVR + ROPE], ca_kr[r0:r0 + 128, :], ["CB" + ss], q="pool")
        C.ld(KB[si][:, :], ca_k[r0:r0 + 128, :], ["KB" + ss], q="pool")
        C.ld(SVB[si][:, :], ca_v[r0:r0 + 128, :], ["SVB" + ss], q="pool")

    for t in range(min(2, NCT)):
        cache_loads(t)
    for t in range(NCT):
        if t + 2 < NCT:
            cache_loads(t + 2)
        si = t % 4
        r0 = t * 128
        C.store(SV[1][r0:r0 + 128, :], SVB[si][:, :], ["SVB%d" % si])
        kv_tail(t % 2, 1, r0, 128, si=si)
        if t == 3:
            load_wkv()
        if t % 2 == 1 and t // 2 < 12:
            ada_group(t // 2)
            ada_done.add(t // 2)
    if (PAST // 128 if alim >= 5 else 0) <= 3:
        load_wkv()
    C.store_q = "pool"
    for g in range(12):
        if g not in ada_done:
            ada_group(g)
    ada_final()
    NTL = len(tiles)
    for step in range(NTL + 2):
        if step < NTL:
            S0(step)
        if 0 <= step - 1 < NTL:
            S1(step - 1)
        if 0 <= step - 2 < NTL:
            S2(step - 2)
    C.finish()
    if nphase == 1:
        return nc

    C = Ctx(nc, PROG)
    HGMAX = 8
    wsq = C.sb("wsq", [128, NCH, HB], BF16)
    wuq = C.sb("wuq", [128, 3, 768], BF16)
    wur = C.sb("wur", [128, 3, 8, 96], BF16)
    mtri = C.sb("mtri", [128, 128], BF16); negu = C.sb("negu", [128, 128], BF16)
    negone = C.sb("negone", [128, 128], BF16); zl = C.sb("zl", [128, 128], BF16); zr = C.sb("zr", [128, 512], BF16)
    hq = C.sb("hq", [128, NCH, 512], BF16); ql = C.sb("ql", [128, 3, 512], BF16)
    cst = C.sb("cst", [96, 512]); snt = C.sb("snt", [96, 512])
    qcat = C.sb("qcat", [97, 8, 512], BF16); sbq = C.sb("sbq", [65, 8, 512], BF16)
    t1 = C.sb("t1", [96, 512]); t2 = C.sb("t2", [96, 512])
    kc = [C.sb("kc%d" % i, [97, HGMAX, 512], BF16) for i in range(3)]
    va = [C.sb("va%d" % i, [128, 4, HGMAX, 128], BF16) for i in range(3)]
    sk = [C.sb("sk%d" % i, [65, HGMAX, 512], BF16) for i in range(3)]
    sv = [C.sb("sv%d" % i, [128, 4, HGMAX, 64], BF16) for i in range(3)]
    Pb = [C.sb("Pb%d" % i, [128, 512], BF16) for i in range(4)]
    Eb = [C.sb("Eb%d" % i, [128, 512]) for i in range(3)]
    SPb = [C.sb("SPb%d" % i, [128, 512], BF16) for i in range(3)]
    Ab = [C.sb("Ab%d" % i, [128, 512], BF16) for i in range(2)]
    Xb = [C.sb("Xb%d" % i, [128, 512]) for i in range(2)]
    Rb = C.sb("Rb", [128, 4, 512], BF16)
    Rs = C.sb("Rs", [128, 8, DEC], BF16)
    rc = C.sb("rc", [64, 512]); ot = [C.sb("ot%d" % i, [64, 512], BF16) for i in range(2)]
    bank = [C.ps("bank%d" % i, [128, 512]) for i in range(8)]
    for c in range(NCH):
        C.ld(wsq[:, c, :], w_in[c * 128:(c + 1) * 128, C_SQ:C_SQ + HB], ["wsq"], q="pool")
    C.memset(wur[:, :, :, :], 0.0, ["wur"])
    for c in range(3):
        C.ld(wuq[:, c, :], w_uq[c * 128:(c + 1) * 128, :], ["wuq"], q="pool")
        src = w_uq[c * 128:(c + 1) * 128, :].rearrange("p (h e) -> p h e", e=96)
        C.ld(wur[:, c, :, 64:80], src[:, :, 80:96], ["wur"], q="pool")
        C.ld(wur[:, c, :, 80:96], src[:, :, 64:80], ["wur"], q="pool")
    C.ts(wur[:, :, :, 64:80], wur[:, :, :, 64:80], -1.0, None, ALU.mult, None, ["wur"], ["wur"])
    C.ld(mtri[:, :], mtri_d[:, :], ["mtri"]); C.ld(negu[:, :], negu_d[:, :], ["negu"])
    C.memset(negone[:, :], -1.0, ["negone"]); C.memset(zl[:, :], 0.0, ["zl"]); C.memset(zr[:, :], 0.0, ["zr"])
    C.ld(qcat[96:97, :, :], ones_d[0:1, :].rearrange("o (h n) -> o h n", h=8), ["qcat"])
    C.ld(sbq[64:65, :, :], ones_d[0:1, :].rearrange("o (h n) -> o h n", h=8), ["sbq"])

    for job in range(2):
        for (col0, nq, nchunk, kind) in qtiles(job):
            HG = 4 if kind == "big" else 8
            GB = 1 if kind == "big" else HG
            Rsel = Rb if kind == "big" else Rs
            C.ld(hq[:, :, 0:nq], HT[job].rearrange("c p n -> p c n")[:, :, col0:col0 + nq], ["hq"])
            C.ld(ql[:, :, 0:nq], QLT[job].rearrange("c p n -> p c n")[:, :, col0:col0 + nq], ["ql"])
            C.ld(cst[64:96, 0:nq], cq[job][:, col0:col0 + nq], ["cst"])
            C.ld(snt[64:96, 0:nq], sq[job][:, col0:col0 + nq], ["snt"])
            for h in range(8):
                pa = bank[(2 * h) % 8]; pb = bank[(2 * h + 1) % 8]
                na = "bank%d" % ((2 * h) % 8); nb_ = "bank%d" % ((2 * h + 1) % 8)
                for c in range(3):
                    C.mm(pa[0:96, 0:nq], wuq[:, c, h * 96:(h + 1) * 96], ql[:, c, 0:nq], c == 0, c == 2, ["wuq", "ql"], [na], inc=(c == 2))
                for c in range(3):
                    C.mm(pb[0:96, 0:nq], wur[:, c, h, :], ql[:, c, 0:nq], c == 0, c == 2, ["wur", "ql"], [nb_], inc=(c == 2))
                C.cp(qcat[0:64, h, 0:nq], pa[0:64, 0:nq], [na], ["qcat"], eng="act")
                C.tt(t1[64:96, 0:nq], pa[64:96, 0:nq], cst[64:96, 0:nq], ALU.mult, [na, "cst"], ["t1"])
                C.tt(t2[64:96, 0:nq], pb[64:96, 0:nq], snt[64:96, 0:nq], ALU.mult, [nb_, "snt"], ["t2"])
                C.tt(qcat[64:96, h, 0:nq], t1[64:96, 0:nq], t2[64:96, 0:nq], ALU.add, ["t1", "t2"], ["qcat"])
            for h in range(8):
                pa = bank[h]; na = "bank%d" % h
                for c in range(NCH):
                    C.mm(pa[0:64, 0:nq], wsq[:, c, h * 64:(h + 1) * 64], hq[:, c, 0:nq], c == 0, c == NCH - 1, ["wsq", "hq"], [na], inc=(c == NCH - 1))
                C.act(sbq[0:64, h, 0:nq], pa[0:64, 0:nq], AF.Identity, [na], ["sbq"], scale=0.125)

            def blocks():
                out = []
                for ch in range(nchunk - 1, -1, -1):
                    if kind == "samp" and ch == nchunk - 1:
                        out.append((ch, 0, DEC, 0, "samp"))
                        continue
                    for kb in range(3, -1, -1):
                        if kind == "big" and ch == nchunk - 1:
                            out.append((ch, kb, 128, 128 * kb, "diag"))
                        elif kind == "halo" and ch == nchunk - 1 and kb == 3:
                            out.append((ch, kb, 128, 0, "halo"))
                        else:
                            out.append((ch, kb, 128, 0, None))
                return out

            blist = blocks()
            obank = lambda hh: (bank[4 + (hh * nq) // 512], "bank%d" % (4 + (hh * nq) // 512), (hh * nq) % 512)
            items0 = []
            for (ch, kb, nk, qa, diag) in blist:
                for b0 in range(0, HG, GB):
                    items0.append((ch, kb, nk, qa, diag, b0))
            chunks_desc = []
            for it_ in items0:
                if not chunks_desc or chunks_desc[-1] != it_[0]:
                    chunks_desc.append(it_[0])

            def load_chunk(att, hg, ch, li):
                k0 = ch * 512
                kw = 512 if not (kind == "samp" and ch == nchunk - 1) else DEC
                hs = slice(hg * HG, (hg + 1) * HG)
                if att == "mla":
                    C.ld(kc[li][0:64, 0:HG, 0:kw], KN[job].rearrange("h d r -> d h r")[:, hs, k0:k0 + kw], ["kc%d_a" % li])
                    C.ld(kc[li][64:96, 0:HG, 0:kw], KRT[job][:, k0:k0 + kw].unsqueeze(1).broadcast_to([32, HG, kw]), ["kc%d_b" % li])
                    C.ld(kc[li][96:97, 0:HG, 0:kw], kbias[job][0:1, k0:k0 + kw].unsqueeze(1).broadcast_to([1, HG, kw]), ["kc%d_c" % li])
                    if kw == 512:
                        C.ld(va[li][:, :, 0:HG, :], VA[job][k0:k0 + 512, :].rearrange("(b p) (h e) -> p b h e", p=128, e=128)[:, :, hs, :], ["va%d" % li])
                    else:
                        C.ld(va[li][0:kw, 0, 0:HG, :], VA[job][k0:k0 + kw, :].rearrange("p (h e) -> p h e", e=128)[:, hs, :], ["va%d" % li])
                else:
                    C.ld(sk[li][0:64, 0:HG, 0:kw], SKT[job].rearrange("h d r -> d h r")[:, hs, k0:k0 + kw], ["sk%d_a" % li])
                    C.ld(sk[li][64:65, 0:HG, 0:kw], kbias[job][0:1, k0:k0 + kw].unsqueeze(1).broadcast_to([1, HG, kw]), ["sk%d_b" % li])
                    if kw == 512:
                        C.ld(sv[li][:, :, 0:HG, :], SV[job][k0:k0 + 512, :].rearrange("(b p) (h e) -> p b h e", p=128, e=64)[:, :, hs, :], ["sv%d" % li])
                    else:
                        C.ld(sv[li][0:kw, 0, 0:HG, :], SV[job][k0:k0 + kw, :].rearrange("p (h e) -> p h e", e=64)[:, hs, :], ["sv%d" % li])

            ldc = 0
            for att in ("mla", "sb"):
                for hg in range(8 // HG):
                    nob = max(1, (HG * nq) // 512)
                    for ob in range(nob):
                        C.mm(bank[4 + ob][:, 0:min(512, HG * nq)], zl[:, :], zr[:, 0:min(512, HG * nq)], True, False, ["zl", "zr"], ["bank%d" % (4 + ob)])
                    if att == "sb":
                        for hh in range(HG):
                            C.memset(Rsel[:, hh, :], 0.0, ["Rb%d" % hh])
                    cbuf = {}
                    for ci, ch in enumerate(chunks_desc):
                        cbuf[ch] = (ldc + ci) % 3
                    ldc += len(chunks_desc)
                    loaded = set()

                    def ensure(ch):
                        if ch not in loaded:
                            loaded.add(ch)
                            load_chunk(att, hg, ch, cbuf[ch])

                    NI = len(items0)
                    first_blk = (items0[0][0], items0[0][1])

                    def geom(ii):
                        (ch, kb, nk, qa, diag, b0) = items0[ii]
                        heads = list(range(b0, b0 + GB))
                        if GB == 1:
                            coff = {b0: 0}
                            c_lo, c_hi = qa, nq
                        else:
                            coff = {hh: (hh - b0) * nq for hh in heads}
                            c_lo, c_hi = 0, GB * nq
                        return ch, kb, nk, qa, diag, heads, coff, c_lo, c_hi, cbuf[ch], slice(kb * 128, kb * 128 + nk)

                    def msk(buf, bname, diag, heads, coff, qa):
                        if diag == "diag":
                            C.tt(buf[:, qa:qa + 128], buf[:, qa:qa + 128], mtri[:, :], ALU.mult, [bname, "mtri"], [bname])
                        elif diag == "halo":
                            for hh in heads:
                                C.tt(buf[:, coff[hh]:coff[hh] + 2], buf[:, coff[hh]:coff[hh] + 2], mtri[:, 126:128], ALU.mult, [bname, "mtri"], [bname])
                        elif diag == "samp":
                            for hh in heads:
                                C.tt(buf[0:DEC, coff[hh]:coff[hh] + DEC], buf[0:DEC, coff[hh]:coff[hh] + DEC], mtri[0:DEC, 0:DEC], ALU.mult, [bname, "mtri"], [bname])

                    def s1(ii):
                        ch, kb, nk, qa, diag, heads, coff, c_lo, c_hi, li, ks = geom(ii)
                        ensure(ch)
                        cpos = chunks_desc.index(ch)
                        if cpos + 1 < len(chunks_desc):
                            ensure(chunks_desc[cpos + 1])
                        if att == "mla":
                            sb_ = bank[ii % 4]; sn = "bank%d" % (ii % 4)
                            pb_ = Pb[ii % 4]; pn = "Pb%d" % (ii % 4)
                            for hh in heads:
                                C.mm(sb_[0:nk, coff[hh] + qa:coff[hh] + nq], kc[li][0:97, hh, ks], qcat[0:97, hg * HG + hh, qa:nq], True, True,
                                     ["kc%d_a" % li, "kc%d_b" % li, "kc%d_c" % li, "qcat"], [sn])
                            C.act(pb_[0:nk, c_lo:c_hi], sb_[0:nk, c_lo:c_hi], AF.Exp, [sn], [pn], scale=MLA_SCALE)
                            if diag == "diag":
                                C.memset(pb_[64:128, qa:qa + 64], 0.0, [pn])
                        else:
                            zb_ = bank[ii % 2]; zn = "bank%d" % (ii % 2)
                            eb_ = Eb[ii % 3]; en = "Eb%d" % (ii % 3)
                            sp_ = SPb[ii % 3]; spn = "SPb%d" % (ii % 3)
                            for hh in heads:
                                C.mm(zb_[0:nk, coff[hh] + qa:coff[hh] + nq], sk[li][0:65, hh, ks], sbq[0:65, hg * HG + hh, qa:nq], True, True,
                                     ["sk%d_a" % li, "sk%d_b" % li, "sbq"], [zn])
                            C.act(eb_[0:nk, c_lo:c_hi], zb_[0:nk, c_lo:c_hi], AF.Exp, [zn], [en])
                            if not SB5:
                                msk(eb_, en, diag, heads, coff, qa)
                            C.act(sp_[0:nk, c_lo:c_hi], eb_[0:nk, c_lo:c_hi], AF.Ln, [en], [spn], bias=1.0)
                            if SB5:
                                msk(sp_, spn, diag, heads, coff, qa)

                    def s2(ii):
                        ch, kb, nk, qa, diag, heads, coff, c_lo, c_hi, li, ks = geom(ii)
                        if att == "mla":
                            pb_ = Pb[ii % 4]; pn = "Pb%d" % (ii % 4)
                            for hh in heads:
                                ob, on, oo = obank(hh)
                                C.mm(ob[:, oo + qa:oo + nq], va[li][0:nk, kb, hh, :], pb_[0:nk, coff[hh] + qa:coff[hh] + nq], False, False,
                                     ["va%d" % li, pn], [on])
                            return
                        first = (ch, kb) == first_blk
                        ab_ = bank[2 + ii % 2]; an = "bank%d" % (2 + ii % 2)
                        sp_ = SPb[ii % 3]; spn = "SPb%d" % (ii % 3)
                        a_ = Ab[ii % 2]; abn = "Ab%d" % (ii % 2)
                        eb_ = Eb[ii % 3]; en = "Eb%d" % (ii % 3)
                        xb_ = Xb[ii % 2]; xn_ = "Xb%d" % (ii % 2)
                        for hh in heads:
                            cs_ = slice(coff[hh] + qa, coff[hh] + nq)
                            if SB5:
                                C.mm(ab_[0:nk, cs_], sk[li][0:65, hh, ks], sbq[0:65, hg * HG + hh, qa:nq], True, False,
                                     ["sk%d_a" % li, "sk%d_b" % li, "sbq"], [an], inc=False)
                            C.mm(ab_[0:nk, cs_], negu[0:nk, 0:nk], sp_[0:nk, cs_], not SB5, first, ["negu", spn], [an], inc=first)
                            if not first:
                                C.mm(ab_[0:nk, cs_], negone[:, 0:nk], Rsel[:, hh, qa:nq], False, True, ["negone", "Rb%d" % hh], [an])
                        if GB == 1:
                            hh = heads[0]
                            C.tt(Rsel[0:nk, hh, qa:nq], Rsel[0:nk, hh, qa:nq], sp_[0:nk, qa:nq], ALU.add, ["Rb%d" % hh, spn], ["Rb%d" % hh], eng="pool")
                        else:
                            rn = ["Rb%d" % hh for hh in heads]
                            C.tt(Rsel[0:nk, :, 0:nq], Rsel[0:nk, :, 0:nq], sp_[0:nk, 0:GB * nq].rearrange("p (h n) -> p h n", n=nq), ALU.add,
                                 rn + [spn], rn, eng="pool")
                        if SB5:
                            C.act(a_[0:nk, c_lo:c_hi], ab_[0:nk, c_lo:c_hi], AF.Exp, [an], [abn])
                            msk(a_, abn, diag, heads, coff, qa)
                        else:
                            C.act(xb_[0:nk, c_lo:c_hi], ab_[0:nk, c_lo:c_hi], AF.Exp, [an], [xn_])
                            C.tt(a_[0:nk, c_lo:c_hi], eb_[0:nk, c_lo:c_hi], xb_[0:nk, c_lo:c_hi], ALU.mult, [en, xn_], [abn])

                    def s3(ii):
                        ch, kb, nk, qa, diag, heads, coff, c_lo, c_hi, li, ks = geom(ii)
                        a_ = Ab[ii % 2]; abn = "Ab%d" % (ii % 2)
                        for hh in heads:
                            ob, on, oo = obank(hh)
                            C.mm(ob[0:64, oo + qa:oo + nq], sv[li][0:nk, kb, hh, :], a_[0:nk, coff[hh] + qa:coff[hh] + nq], False, False,
                                 ["sv%d" % li, abn], [on])

                    if att == "mla":
                        LAG = 2
                        for step in range(NI + LAG):
                            if step < NI:
                                s1(step)
                            if 0 <= step - LAG < NI:
                                s2(step - LAG)
                    else:
                        for step in range(NI + 2):
                            if step < NI:
                                s1(step)
                            if 0 <= step - 1 < NI:
                                s2(step - 1)
                            if 0 <= step - 2 < NI:
                                s3(step - 2)
                    for hh in range(HG):
                        ob, on, oo = obank(hh)
                        oi = hh % 2
                        if att == "mla":
                            C.ts(rc[:, 0:nq], ob[64:128, oo:oo + nq], 1e-30, None, ALU.max, None, [on], ["rc"])
                            C.recip(rc[:, 0:nq], rc[:, 0:nq], ["rc"], ["rc"])
                            C.tt(ot[oi][:, 0:nq], ob[0:64, oo:oo + nq], rc[:, 0:nq], ALU.mult, [on, "rc"], ["ot%d" % oi])
                            hidx = hg * HG + hh
                        else:
                            C.cp(ot[oi][:, 0:nq], ob[0:64, oo:oo + nq], [on], ["ot%d" % oi], eng="act")
                            hidx = 8 + hg * HG + hh
                        C.store(OT[job][hidx, :, col0:col0 + nq], ot[oi][:, 0:nq], ["ot%d" % oi])
    C.finish()
    if nphase == 2:
        return nc

    C = Ctx(nc, PROG)
    wg = C.sb("wg", [128, NCH, 2048], BF16)
    wpa = C.sb("wpa", [128, 4, D], BF16); wpb = C.sb("wpb", [128, 4, D], BF16)
    wo = C.sb("wo", [128, NCH, D], BF16)
    gt1 = [C.sb("gt1_%d" % j, [128, D]) for j in range(2)]
    hq = C.sb("hq", [128, NCH, 512], BF16); oq = C.sb("oq", [128, 8, 512], BF16)
    sig = C.sb("sig", [128, 16, 512], BF16); mg = C.sb("mg", [128, NCH, 512], BF16)
    ta = [C.sb("ta%d" % i, [128, 512]) for i in range(2)]; tb = [C.sb("tb%d" % i, [128, 512]) for i in range(2)]
    xo = [C.sb("xo%d" % i, [128, D]) for i in range(2)]
    mo = [C.sb("mo%d" % i, [128, D]) for i in range(2)]
    stat = [C.sb("stat%d" % i, [128, 8]) for i in range(2)]
    junk = C.sb("junk", [128, D], BF16)
    bank = [C.ps("bank%d" % i, [128, 512]) for i in range(8)]
    for c in range(NCH):
        C.ld(wg[:, c, :], w_in[c * 128:(c + 1) * 128, C_GA:C_GA + 2048], ["wg"], q="pool")
        C.ld(wo[:, c, :], w_out[c * 128:(c + 1) * 128, :], ["wo"], q="pool")
        if c < 4:
            C.ld(wpa[:, c, :], w_pa[c * 128:(c + 1) * 128, :], ["wpa"], q="pool")
            C.ld(wpb[:, c, :], w_pb[c * 128:(c + 1) * 128, :], ["wpb"], q="pool")
    for j in range(2):
        C.ld(gt1[j][:, :], MOD[j, 4:5, :].partition_broadcast(128), ["gt1"])
    for c in range(NCH):
        for ab in range(2):
            C.ld(WUPB[:, :, c, ab, :].rearrange("pr p n -> p pr n"),
                 w_up[c * 128:(c + 1) * 128, ab * DFF:(ab + 1) * DFF].rearrange("p (pr n) -> p pr n", n=128), ["wupb%d_%d" % (c, ab)], q="pool")
    it = 0
    for job in range(2):
        for (col0, nq, nchunk, kind) in qtiles(job):
            C.ld(hq[:, :, 0:nq], HT[job].rearrange("c p n -> p c n")[:, :, col0:col0 + nq], ["hq"])
            C.ld(oq[:, :, 0:nq], OT[job].rearrange("(pr two) d n -> (two d) pr n", two=2)[:, :, col0:col0 + nq], ["oq"])
            for cc in range(16):
                pb_ = bank[cc % 2]; pn = "bank%d" % (cc % 2)
                for c in range(NCH):
                    C.mm(pb_[:, 0:nq], wg[:, c, cc * 128:(cc + 1) * 128], hq[:, c, 0:nq], c == 0, c == NCH - 1, ["wg", "hq"], [pn], inc=(c == NCH - 1))
                C.act(sig[:, cc, 0:nq], pb_[:, 0:nq], AF.Sigmoid, [pn], ["sig"])
            for cc in range(NCH):
                i2 = cc % 2
                pa_ = bank[2 + 2 * i2]; pan = "bank%d" % (2 + 2 * i2)
                pb_ = bank[3 + 2 * i2]; pbn = "bank%d" % (3 + 2 * i2)
                for h in range(4):
                    C.mm(pa_[:, 0:nq], wpa[:, h, cc * 128:(cc + 1) * 128], oq[:, h, 0:nq], h == 0, h == 3, ["wpa", "oq"], [pan], inc=(h == 3))
                for h in range(4):
                    C.mm(pb_[:, 0:nq], wpb[:, h, cc * 128:(cc + 1) * 128], oq[:, 4 + h, 0:nq], h == 0, h == 3, ["wpb", "oq"], [pbn], inc=(h == 3))
                C.tt(ta[i2][:, 0:nq], pa_[:, 0:nq], sig[:, cc, 0:nq], ALU.mult, [pan, "sig"], ["ta%d" % i2])
                C.tt(tb[i2][:, 0:nq], pb_[:, 0:nq], sig[:, 8 + cc, 0:nq], ALU.mult, [pbn, "sig"], ["tb%d" % i2])
                C.tt(mg[:, cc, 0:nq], ta[i2][:, 0:nq], tb[i2][:, 0:nq], ALU.add, ["ta%d" % i2, "tb%d" % i2], ["mg"])
            for s0 in range(0, nq, 128):
                ns = min(128, nq - s0)
                i2 = it % 2
                it += 1
                s = "%d" % i2
                xrow = own_bufrow(col0 + s0) if job == 0 else s0
                xsrc = xk if job == 0 else x_s
                C.ld(xo[i2][0:ns, :], xsrc[xrow:xrow + ns, :], ["xo" + s])
                for half in range(2):
                    pb_ = bank[6 + half]; pn = "bank%d" % (6 + half)
                    for cc in range(NCH):
                        C.mm(pb_[0:ns, :], mg[:, cc, s0:s0 + ns], wo[:, cc, half * 512:(half + 1) * 512], cc == 0, cc == NCH - 1, ["mg", "wo"], [pn], inc=(cc == NCH - 1))
                    C.act(junk[0:ns, 0:512], pb_[0:ns, :], AF.Square, [pn], ["junk", "stat" + s], accum=stat[i2][0:ns, half:half + 1])
                    C.tt(mo[i2][0:ns, half * 512:(half + 1) * 512], pb_[0:ns, :], gt1[job][0:ns, half * 512:(half + 1) * 512], ALU.mult, [pn, "gt1"], ["mo" + s])
                C.tt(stat[i2][0:ns, 2:3], stat[i2][0:ns, 0:1], stat[i2][0:ns, 1:2], ALU.add, ["stat" + s], ["stat" + s])
                C.rstd(stat[i2], ns, 2, D, "stat" + s)
                C.stt(xo[i2][0:ns, :], mo[i2][0:ns, :], stat[i2][0:ns, 3:4], xo[i2][0:ns, :], ALU.mult, ALU.add, ["mo" + s, "stat" + s, "xo" + s], ["xo" + s])
                C.store(X1[job][col0 + s0:col0 + s0 + ns, :], xo[i2][0:ns, :], ["xo" + s])
    C.finish()
    if nphase == 3:
        return nc

    C = Ctx(nc, PROG)
    wdn = C.sb("wdn", [128, NFC, D], BF16)
    wu = [C.sb("wu%d" % i, [128, NCH, 2, 128], BF16) for i in range(3)]
    idb = C.sb("idb", [128, 128], BF16)
    gt2 = [C.sb("gt2_%d" % j, [128, D]) for j in range(2)]
    m2 = C.sb("m2", [128, 2, 2, NCH])
    cw = C.sb("cw", [128, 3, 44]); cb = C.sb("cb", [128, 44]); hv = C.sb("hv", [128, 2])
    UC = C.sb("UC", [128, 44, 2])
    h2 = [C.sb("h2_%d" % i, [128, NCH, 512], BF16) for i in range(2)]
    G = C.sb("G", [128, NFC, 512], BF16)
    x1t = [C.sb("x1t%d" % i, [128, D]) for i in range(2)]
    xn = [C.sb("xn%d" % i, [128, D], BF16) for i in range(2)]
    mo = [C.sb("mo%d" % i, [128, D]) for i in range(2)]
    stat = [C.sb("stat%d" % i, [128, 8]) for i in range(2)]
    junk = C.sb("junk", [128, D], BF16)
    Ua = [C.sb("Ua%d" % i, [128, 516]) for i in range(2)]; Ub = [C.sb("Ub%d" % i, [128, 516]) for i in range(2)]
    Ta = [C.sb("Ta%d" % i, [128, 512]) for i in range(3)]; Tb = [C.sb("Tb%d" % i, [128, 512]) for i in range(3)]
    W1 = [C.sb("W1%d" % i, [128, 512]) for i in range(2)]; W2 = [C.sb("W2%d" % i, [128, 512]) for i in range(2)]
    tp = C.ps("tp", [128, NCH, 128], BF16)
    pua = [C.ps("pua%d" % i, [128, 512]) for i in range(2)]; pub = [C.ps("pub%d" % i, [128, 512]) for i in range(2)]
    pf = [C.ps("pf%d" % i, [128, 512]) for i in range(2)]
    for c in range(NFC):
        C.ld(wdn[:, c, :], w_dn[c * 128:(c + 1) * 128, :], ["wdn"], q="pool")
    C.ld(idb[:, :], ident[:, :], ["idb"])
    for j in range(2):
        C.ld(gt2[j][:, :], MOD[j, 5:6, :].partition_broadcast(128), ["gt2"])
        C.ld(m2[:, j, 0, :], fm(MOD[j, 2]), ["m2"], slow=True)
        C.ld(m2[:, j, 1, :], fm(MOD[j, 3]), ["m2"], slow=True)
    for k in range(3):
        C.ld(cw[:, k, :], fm(conv_w[k]), ["cw"], slow=True)
    C.ld(cb[:, :], fm(conv_b), ["cb"], slow=True)
    C.ld(hv[:, :], hv_d[0:1, :].partition_broadcast(128), ["hv"])
    UCALL = ["UC%d" % p for p in range(NFC)]
    C.memset(UC[:, :, :], 0.0, UCALL)
    itc = [0]
    wbase = 0
    alltiles = [(job,) + t for job in range(2) for t in qtiles(job)]

    def h2_stage(k):
        (job, col0, nq, nchunk, kind) = alltiles[k]
        hb = h2[k % 2]; hn = "h2_%d" % (k % 2)
        for s0 in range(0, nq, 128):
            ns = min(128, nq - s0)
            i2 = itc[0] % 2
            itc[0] += 1
            s = "%d" % i2
            C.ld(x1t[i2][0:ns, :], X1[job][col0 + s0:col0 + s0 + ns, :], ["x1t" + s])
            C.act(junk[0:ns, :], x1t[i2][0:ns, :], AF.Square, ["x1t" + s], ["junk", "stat" + s], accum=stat[i2][0:ns, 0:1])
            C.rstd(stat[i2], ns, 0, D, "stat" + s)
            C.ts(xn[i2][0:ns, :], x1t[i2][0:ns, :], stat[i2][0:ns, 1:2], None, ALU.mult, None, ["x1t" + s, "stat" + s], ["xn" + s])
            for c in range(NCH):
                C.tr(tp[:, c, 0:ns], xn[i2][0:ns, c * 128:(c + 1) * 128], idb[0:ns, 0:ns], ["xn" + s, "idb"], ["tp"], inc=(c == NCH - 1))
            for c in range(NCH):
                if c % 2 == 0:
                    C.act(hb[:, c, s0:s0 + ns], tp[:, c, 0:ns], AF.Identity, ["tp", "m2"], [hn], scale=m2[:, job, 0, c:c + 1], bias=m2[:, job, 1, c:c + 1])
                else:
                    C.ts(hb[:, c, s0:s0 + ns], tp[:, c, 0:ns], m2[:, job, 0, c:c + 1], m2[:, job, 1, c:c + 1], ALU.mult, ALU.add, ["tp", "m2"], [hn])

    h2_stage(0)
    for k, (job, col0, nq, nchunk, kind) in enumerate(alltiles):
        if True:
            if job == 1 and alltiles[k - 1][0] == 0:
                for t in range(2):
                    C.ld(UC[:, :, t], fm(st_conv[t]), UCALL, slow=True)
            h2b = h2[k % 2]; h2n = "h2_%d" % (k % 2)
            hrun = 1 if (job == 0 and col0 >= RUN_COL0[1]) else 0
            ucn = lambda p: "UC%d" % p

            def ldw(p):
                w3 = (wbase + p) % 3
                C.ld(wu[w3][:, :, :, :], WUPB[p], ["wu%d" % w3])

            def t1(p):
                w3 = (wbase + p) % 3
                i2 = p % 2
                s = "%d" % i2
                wn = "wu%d" % w3
                if p + 1 < NFC:
                    ldw(p + 1)
                for (ab, pu, pun, U, un, fc) in ((0, pua[i2], "pua" + s, Ua[i2], "Ua" + s, p), (1, pub[i2], "pub" + s, Ub[i2], "Ub" + s, NFC + p)):
                    for c in range(NCH):
                        C.mm(pu[:, 0:nq], wu[w3][:, c, ab, :], h2b[:, c, 0:nq], c == 0, c == NCH - 1, [wn, h2n], [pun], inc=(c == NCH - 1))
                    C.act(U[:, 0:2], UC[:, fc, :], AF.Copy, [ucn(p)], [un + "c"])
                    C.act(U[:, 2:2 + nq], pu[:, 0:nq], AF.Copy, [pun], [un])
                    if kind == "halo":
                        C.act(UC[:, fc, :], U[:, 2:4], AF.Copy, [un, "hv"], [ucn(p)], scale=hv[:, hrun:hrun + 1])
                    else:
                        C.act(UC[:, fc, :], U[:, nq:nq + 2], AF.Copy, [un, un + "c"], [ucn(p)])

            def t2(p):
                i2 = p % 2
                s = "%d" % i2
                i3 = "%d" % (p % 3)
                for (U, un, T, tn, fc) in ((Ua[i2], "Ua" + s, Ta[p % 3], "Ta" + i3, p), (Ub[i2], "Ub" + s, Tb[p % 3], "Tb" + i3, NFC + p)):
                    C.ts(T[:, 0:nq], U[:, 2:2 + nq], cw[:, 2, fc:fc + 1], cb[:, fc:fc + 1], ALU.mult, ALU.add, [un, "cw", "cb"], [tn], eng="pool")
                    C.stt(T[:, 0:nq], U[:, 1:1 + nq], cw[:, 1, fc:fc + 1], T[:, 0:nq], ALU.mult, ALU.add, [un, un + "c", "cw", tn], [tn])
                    C.stt(T[:, 0:nq], U[:, 0:nq], cw[:, 0, fc:fc + 1], T[:, 0:nq], ALU.mult, ALU.add, [un, un + "c", "cw", tn], [tn])

            def t4(p):
                i3 = "%d" % (p % 3)
                s = "%d" % (p % 2)
                A_ = Ta[p % 3]; B_ = Tb[p % 3]; w2_ = W2[p % 2]
                C.act(w2_[:, 0:nq], A_[:, 0:nq], AF.Gelu_apprx_tanh, ["Ta" + i3], ["W2" + s])
                C.tt(G[:, p, 0:nq], w2_[:, 0:nq], B_[:, 0:nq], ALU.mult, ["W2" + s, "Tb" + i3], ["G"])

            ldw(0)
            for st_ in range(NFC + 3):
                if kind != "halo":
                    if 0 <= st_ - 1 < NFC:
                        t2(st_ - 1)
                    if 0 <= st_ - 2 < NFC:
                        t4(st_ - 2)
                if st_ < NFC:
                    t1(st_)
            wbase += NFC
            if k + 1 < len(alltiles):
                h2_stage(k + 1)
            last_of_job = (k + 1 == len(alltiles)) or (alltiles[k + 1][0] != job)
            for s0 in (range(0, nq, 128) if kind != "halo" else []):
                ns = min(128, nq - s0)
                i2 = itc[0] % 2
                itc[0] += 1
                s = "%d" % i2
                C.ld(x1t[i2][0:ns, :], X1[job][col0 + s0:col0 + s0 + ns, :], ["x1t" + s])
                for half in range(2):
                    for p in range(NFC):
                        C.mm(pf[half][0:ns, :], G[:, p, s0:s0 + ns], wdn[:, p, half * 512:(half + 1) * 512], p == 0, p == NFC - 1, ["G", "wdn"], ["pf%d" % half], inc=(p == NFC - 1))
                    C.act(junk[0:ns, 0:512], pf[half][0:ns, :], AF.Square, ["pf%d" % half], ["junk", "stat" + s], accum=stat[i2][0:ns, 4 + half:5 + half])
                    C.tt(mo[i2][0:ns, half * 512:(half + 1) * 512], pf[half][0:ns, :], gt2[job][0:ns, half * 512:(half + 1) * 512], ALU.mult, ["pf%d" % half, "gt2"], ["mo" + s])
                C.tt(stat[i2][0:ns, 6:7], stat[i2][0:ns, 4:5], stat[i2][0:ns, 5:6], ALU.add, ["stat" + s], ["stat" + s])
                C.rstd(stat[i2], ns, 6, D, "stat" + s)
                C.stt(x1t[i2][0:ns, :], mo[i2][0:ns, :], stat[i2][0:ns, 7:8], x1t[i2][0:ns, :], ALU.mult, ALU.add, ["mo" + s, "stat" + s, "x1t" + s], ["x1t" + s])
                orow = own_outrow(col0 + s0) if job == 0 else s0
                C.store(y[job][orow:orow + ns, :], x1t[i2][0:ns, :], ["x1t" + s])
            if last_of_job:
                for t in range(2):
                    C.store(fm(o_conv[job][t]), UC[:, :, t], UCALL, slow=True)
    C.finish()
    return nc


_NC = None


def _rope_cs(pos):
    inv = (10000.0 ** (-np.arange(0, ROPE, 2, dtype=np.float32) / ROPE)).astype(np.float32)
    ang = pos.astype(np.float32)[:, None] * inv[None, :]
    return np.cos(ang).astype(np.float32), np.sin(ang).astype(np.float32)


def _tables(pos):
    c, s = _rope_cs(pos)
    tm = np.ascontiguousarray(np.concatenate([c, s], axis=1))
    return tm, np.ascontiguousarray(np.concatenate([c, c], axis=1).T), np.ascontiguousarray(np.concatenate([s, s], axis=1).T)


def kernel(x_prompt, x_sample, cache_mla_ckv, cache_mla_krope, cache_sb_k, cache_sb_v, state_ffn_conv,
           c_prompt, c_sample, w_ada, b_ada, g_pre_mix, g_post_mix, g_pre_ffn, g_post_ffn,
           w_in, g_q_lat, w_uq, g_kv_lat, w_uk, w_uv, w_proj_a, w_proj_b, w_out,
           w_up, conv_w, conv_b, w_down):
    global _NC
    f = lambda a: np.ascontiguousarray(np.asarray(a, dtype=np.float32))
    bf = ml_dtypes.bfloat16
    x_prompt, x_sample = f(x_prompt), f(x_sample)
    B, T = x_prompt.shape[0], x_prompt.shape[1]
    if _NC is None:
        _NC = build_nc()
    ii = np.arange(128)
    consts = {
        "ident": np.eye(128, dtype=np.float32).astype(bf),
        "mtri": (ii[:, None] < ii[None, :]).astype(np.float32).astype(bf),
        "negu": (-(ii[:, None] >= ii[None, :]).astype(np.float32)).astype(bf),
        "ones": np.ones((1, 4096), np.float32).astype(bf),
    }
    shared = {
        "w_ada": f(w_ada)[0], "b_ada": f(b_ada)[0], "g_pre": f(g_pre_mix)[0], "g_post": f(g_post_mix)[0],
        "g_pre2": f(g_pre_ffn)[0], "g_post2": f(g_post_ffn)[0], "g_kv": f(g_kv_lat), "g_q": f(g_q_lat)[0],
        "w_in": f(w_in)[0], "w_uq": f(w_uq)[0], "w_uk": f(w_uk)[0], "w_uv": f(w_uv)[0],
        "w_pa": f(w_proj_a)[0], "w_pb": f(w_proj_b)[0], "w_out": f(w_out)[0],
        "w_up": f(w_up)[0], "conv_w": f(conv_w)[0], "conv_b": f(conv_b)[0], "w_dn": f(w_down)[0],
    }
    cs_s, cq_s, sq_s = _tables(PAST + np.arange(DEC))
    in_maps = []
    for c in range(8):
        b, j = c // 4, c % 4
        off = RUN_ROW0[0] - 1024 * j
        nvalid = NBUF - off
        xk = np.zeros((NBUF, D), np.float32)
        xk[off:] = x_prompt[b, 0:nvalid]
        pos = np.maximum(np.arange(NBUF) - off, 0)
        cs_p, _, _ = _tables(pos)
        own_rows = np.array([own_bufrow(c_) for c_ in range(NOWN)])
        _, cq_p, sq_p = _tables(pos[own_rows])
        kb_p = np.where(np.arange(NBUF) >= off, 0.0, NEGBIG).astype(np.float32)[None, :].astype(bf)
        m = dict(shared)
        m.update(consts)
        m.update({
            "xk": xk, "x_s": np.ascontiguousarray(x_sample[c]),
            "c2": np.ascontiguousarray(np.stack([f(c_prompt)[b], f(c_sample)[c]])),
            "cs_p": cs_p, "cs_s": cs_s, "cq_p": cq_p, "cq_s": cq_s, "sq_p": sq_p, "sq_s": sq_s,
            "kb_p": kb_p, "kb_s": np.zeros((1, SLEN), np.float32).astype(bf),
            "hv": np.array([[1.0 if j > 0 else 0.0, 1.0]], np.float32),
            "ca_ckv": f(cache_mla_ckv)[0, c], "ca_kr": f(cache_mla_krope)[0, c],
            "ca_k": f(cache_sb_k)[0, c].reshape(PAST, HB), "ca_v": f(cache_sb_v)[0, c].reshape(PAST, HB),
            "st_conv": f(state_ffn_conv)[0, c],
        })
        in_maps.append(m)
    res = run_bass_kernel_spmd(_NC, in_maps, core_ids=list(range(8)))
    rs = res.results

    def gp(name, w):
        out = np.zeros((B, T, w), np.float32)
        for b in range(B):
            for j in range(4):
                v = rs[b * 4 + j][name]
                out[b, 1024 * j:1024 * j + 1024] = v[0:1024]
                out[b, 4096 + 1024 * j:4096 + 1024 * j + 1024] = v[1024:2048]
        return out

    def gs(name, w):
        return np.stack([rs[c][name] for c in range(8)]).reshape(8, DEC, w)

    conv_p = np.stack([rs[b * 4 + 3]["o_conv_p"] for b in range(B)])[None]
    conv_s = np.stack([rs[c]["o_conv_s"] for c in range(8)])[None]
    return (gp("y_p", D), gs("y_s", D),
            gp("o_ckv_p", KVR)[None], gp("o_kr_p", ROPE)[None],
            gp("o_k_p", HB).reshape(1, B, T, 8, 64), gp("o_v_p", HB).reshape(1, B, T, 8, 64), conv_p,
            gs("o_ckv_s", KVR)[None], gs("o_kr_s", ROPE)[None],
            gs("o_k_s", HB).reshape(1, 8, DEC, 8, 64), gs("o_v_s", HB).reshape(1, 8, DEC, 8, 64), conv_s)
```

```python
import numpy as np
import concourse.bass as bass
import concourse.mybir as mybir
from concourse.bass_utils import run_bass_kernel_spmd

F32 = mybir.dt.float32
BF16 = mybir.dt.bfloat16
F32R = mybir.dt.float32r
I32 = mybir.dt.int32
AF = mybir.ActivationFunctionType
ALU = mybir.AluOpType

ENGS = ("pe", "act", "dve", "pool", "sp")
DPOOL = {"sp": 28, "pool": 12, "act": 2}


class Res:
    __slots__ = ("w", "r", "name", "x")

    def __init__(self, name="", x=False):
        self.w = None
        self.r = []
        self.name = name
        self.x = x


class Prog:
    def __init__(self):
        self.q = {e: [] for e in ENGS}
        self.cnt = {e: 0 for e in ENGS}
        self.seen = {e: {} for e in ENGS}
        self.dma_cnt = {}
        self.dma_gen = {}

    def _deps(self, eng, reads, writes):
        toks = []
        for r in reads:
            if r.w is not None:
                toks.append(r.w)
            if r.x:
                toks.extend(t for t in r.r if t[0] != eng)
        for w in writes:
            if w.w is not None:
                toks.append(w.w)
            toks.extend(w.r)
        waits = {}
        seen = self.seen[eng]
        for (k, v) in toks:
            if k == eng and eng == "pe":
                continue
            if v > seen.get(k, 0) and v > waits.get(k, 0):
                waits[k] = v
        for k, v in waits.items():
            seen[k] = v
        return waits

    def _mark(self, tok, reads, writes):
        for r in reads:
            r.r.append(tok)
        for w in writes:
            w.w = tok
            w.r = []

    def op(self, eng, fn, reads=(), writes=(), inc=True):
        waits = self._deps(eng, reads, writes)
        if inc:
            self.cnt[eng] += 1
            tok = (eng, self.cnt[eng])
        else:
            tok = (eng, self.cnt[eng] + 1)
        self.q[eng].append((waits, fn, (eng, 1) if inc else None))
        self._mark(tok, reads, writes)
        return tok

    def dma(self, eng, fn, reads=(), writes=()):
        npool = DPOOL[eng]
        idx = self.dma_cnt.get(eng, 0)
        self.dma_cnt[eng] = idx + 1
        key = ("d", eng, idx % npool)
        waits = self._deps(eng, reads, writes)
        g = self.dma_gen.get(key, 0)
        if g > 0:
            if self.seen[eng].get(key, 0) < 16 * g:
                waits[key] = 16 * g
                self.seen[eng][key] = 16 * g
        self.dma_gen[key] = g + 1
        tok = (key, 16 * (g + 1))
        self.q[eng].append((waits, fn, (key, 16)))
        self._mark(tok, reads, writes)
        return tok

    def barrier(self, allres=()):
        toks = [(e, self.cnt[e]) for e in ENGS if self.cnt[e] > 0]
        toks += [(k, 16 * g) for k, g in self.dma_gen.items() if g > 0]
        for e in ENGS:
            waits = {}
            for (k, v) in toks:
                if k == e:
                    continue
                if v > self.seen[e].get(k, 0):
                    waits[k] = v
                    self.seen[e][k] = v
            if waits:
                self.q[e].append((waits, None, None))

    def make_sems(self, nc, stack):
        sems = {}
        for e in ENGS:
            sems[e] = stack.enter_context(nc.semaphore("s_" + e))
        for e, n in DPOOL.items():
            for k in range(n):
                sems[("d", e, k)] = stack.enter_context(nc.semaphore("s_d%s%d" % (e, k)))
        self.sems = sems

    def emit(self, nc, stack):
        sems = self.sems
        self.barrier()
        block = stack.enter_context(nc.Block())
        q = self.q
        self.q = {e: [] for e in ENGS}

        def run(engobj, name):
            for (waits, fn, inc) in q[name]:
                for k, v in waits.items():
                    engobj.wait_ge(sems[k], v)
                if fn is None:
                    continue
                ins = fn(engobj)
                if inc is not None:
                    ins.then_inc(sems[inc[0]], inc[1])

        @block.sync
        def _(e):
            run(e, "sp")

        @block.tensor
        def _(e):
            run(e, "pe")

        @block.scalar
        def _(e):
            run(e, "act")

        @block.vector
        def _(e):
            run(e, "dve")

        @block.gpsimd
        def _(e):
            run(e, "pool")


from contextlib import ExitStack
import ml_dtypes

D = 1024
NCH = 8
QR = 384
KVR = 256
ROPE = 32
HB = 512
DFF = 2816
NFC = 22
EPS = 1e-6
NBUF = 8192
OWN0 = 6144
RUN_ROW0 = (3072, 7168)
RUN_COL0 = (0, 1026)
ROWS = 2048
NOWN = ROWS + 4


def own_bufrow(col):
    r = 0 if col < RUN_COL0[1] else 1
    return RUN_ROW0[r] - 2 + (col - RUN_COL0[r])


def own_outrow(col):
    return col - 2 if col < RUN_COL0[1] else col - 4

DEC = 32
PAST = 4096
SLEN = 4224
MLA_SCALE = float((64 + 32) ** -0.5)
NEGBIG = -30000.0
SB5 = True
C_QL, C_CKV, C_KR, C_SQ, C_SK, C_SV, C_GA, C_GB = 0, 384, 640, 672, 1184, 1696, 2208, 3232


class Ctx:
    _n = 0

    def __init__(self, nc, prog):
        self.nc = nc
        Ctx._n += 1
        self.pfx = "p%d_" % Ctx._n
        self.st = ExitStack()
        self.P = prog
        self.R = {}
        self.psn = set()
        self.store_q = "sp"

    def sb(self, name, shape, dt=F32):
        return self.st.enter_context(self.nc.sbuf_tensor(self.pfx + name, list(shape), dt))

    def ps(self, name, shape, dt=F32):
        self.psn.add(name)
        return self.st.enter_context(self.nc.psum_tensor(self.pfx + name, list(shape), dt))

    def r(self, names):
        out = []
        for n in names:
            if n not in self.R:
                self.R[n] = Res(n, n in self.psn)
            out.append(self.R[n])
        return out

    def mm(self, out, lhsT, rhs, start, stop, rd, wr, inc=True):
        self.P.op("pe", lambda e: e.matmul(out, lhsT=lhsT, rhs=rhs, start=start, stop=stop, skip_group_check=True),
                  reads=self.r(rd), writes=self.r(wr), inc=inc)

    def tr(self, out, in_, ident, rd, wr, inc=True):
        self.P.op("pe", lambda e: e.transpose(out=out, in_=in_, identity=ident), reads=self.r(rd), writes=self.r(wr), inc=inc)

    def act(self, out, in_, func, rd, wr, scale=None, bias=None, accum=None):
        kw = {}
        if scale is not None:
            kw["scale"] = scale
        if bias is not None:
            kw["bias"] = bias
        if accum is not None:
            kw["accum_out"] = accum
        self.P.op("act", lambda e: e.activation(out=out, in_=in_, func=func, **kw), reads=self.r(rd), writes=self.r(wr))

    def tt(self, out, in0, in1, op, rd, wr, eng="dve"):
        self.P.op(eng, lambda e: e.tensor_tensor(out=out, in0=in0, in1=in1, op=op), reads=self.r(rd), writes=self.r(wr))

    def ts(self, out, in0, s1, s2, op0, op1, rd, wr, eng="dve"):
        if op1 is None:
            self.P.op(eng, lambda e: e.tensor_scalar(out=out, in0=in0, scalar1=s1, scalar2=None, op0=op0), reads=self.r(rd), writes=self.r(wr))
        else:
            self.P.op(eng, lambda e: e.tensor_scalar(out=out, in0=in0, scalar1=s1, scalar2=s2, op0=op0, op1=op1), reads=self.r(rd), writes=self.r(wr))

    def stt(self, out, in0, scalar, in1, op0, op1, rd, wr):
        self.P.op("dve", lambda e: e.scalar_tensor_tensor(out=out, in0=in0, scalar=scalar, in1=in1, op0=op0, op1=op1),
                  reads=self.r(rd), writes=self.r(wr))

    def cp(self, out, in_, rd, wr, eng="dve"):
        if eng == "act":
            self.act(out, in_, AF.Identity, rd, wr)
        else:
            self.P.op(eng, lambda e: e.tensor_copy(out=out, in_=in_), reads=self.r(rd), writes=self.r(wr))

    def memset(self, ap, val, wr, eng="dve"):
        self.P.op(eng, lambda e: e.memset(ap, val), writes=self.r(wr))

    def recip(self, out, in_, rd, wr):
        self.P.op("dve", lambda e: e.reciprocal(out=out, in_=in_), reads=self.r(rd), writes=self.r(wr))

    def ld(self, out, in_, wr, q="sp", rd=(), slow=False):
        if slow:
            self.P.dma(q, lambda e: e.dma_start(out=out, in_=in_, allow_slow_non_contiguous=True), reads=self.r(rd), writes=self.r(wr))
        else:
            self.P.dma(q, lambda e: e.dma_start(out=out, in_=in_), reads=self.r(rd), writes=self.r(wr))

    def store(self, out, in_, rd, q=None, slow=False):
        q = q or self.store_q
        if slow:
            self.P.dma(q, lambda e: e.dma_start(out=out, in_=in_, allow_slow_non_contiguous=True), reads=self.r(rd))
        else:
            self.P.dma(q, lambda e: e.dma_start(out=out, in_=in_), reads=self.r(rd))

    def rstd(self, stat, n, col, width, rname):
        self.act(stat[0:n, col + 1:col + 2], stat[0:n, col:col + 1], AF.Ln, [rname], [rname], scale=1.0 / width, bias=EPS)
        self.act(stat[0:n, col + 1:col + 2], stat[0:n, col + 1:col + 2], AF.Exp, [rname], [rname], scale=-0.5)

    def finish(self):
        self.P.emit(self.nc, self.st)
        self.st.close()


def fm(vec, p=128):
    return vec.rearrange("(c p) -> p c", p=p)


def qtiles(job):
    if job == 0:
        out = []
        for r in range(2):
            ch0 = RUN_ROW0[r] // 512
            out.append((RUN_COL0[r], 2, ch0, "halo"))
            for k in range(2):
                out.append((RUN_COL0[r] + 2 + 512 * k, 512, ch0 + k + 1, "big"))
        return out
    return [(0, DEC, 9, "samp")]


def build_nc(debug=False, nphase=4, alim=9, ntl=None):
    nc = bass.Bass("TRN2", target_bir_lowering=False)
    din = lambda n, s, d=F32: nc.dram_tensor(n, list(s), d, kind="ExternalInput").ap()
    dout = lambda n, s: nc.dram_tensor(n, list(s), F32, kind="ExternalOutput").ap()
    dscr = lambda n, s, d=BF16: nc.dram_tensor(n, list(s), d, kind=("ExternalOutput" if debug else "Internal")).ap()

    xk = din("xk", [NBUF, D]); x_s = din("x_s", [DEC, D]); c2 = din("c2", [2, D])
    w_ada = din("w_ada", [D, 6 * D]); b_ada = din("b_ada", [6 * D])
    g_pre = din("g_pre", [D]); g_post = din("g_post", [D]); g_pre2 = din("g_pre2", [D]); g_post2 = din("g_post2", [D])
    g_kv = din("g_kv", [1, KVR]); g_q = din("g_q", [QR])
    w_in = din("w_in", [D, 4256]); w_uq = din("w_uq", [QR, 768]); w_uk = din("w_uk", [KVR, 512]); w_uv = din("w_uv", [KVR, 512])
    w_pa = din("w_pa", [512, D]); w_pb = din("w_pb", [512, D]); w_out = din("w_out", [D, D])
    w_up = din("w_up", [D, 2 * DFF]); conv_w = din("conv_w", [3, 2 * DFF]); conv_b = din("conv_b", [2 * DFF]); w_dn = din("w_dn", [DFF, D])
    cs_p = din("cs_p", [NBUF, 32]); cs_s = din("cs_s", [DEC, 32])
    cq = [din("cq_p", [32, NOWN]), din("cq_s", [32, DEC])]
    sq = [din("sq_p", [32, NOWN]), din("sq_s", [32, DEC])]
    kbias = [din("kb_p", [1, NBUF], BF16), din("kb_s", [1, SLEN], BF16)]
    ident = din("ident", [128, 128], BF16); mtri_d = din("mtri", [128, 128], BF16); negu_d = din("negu", [128, 128], BF16)
    ones_d = din("ones", [1, 4096], BF16); hv_d = din("hv", [1, 2])
    ca_ckv = din("ca_ckv", [PAST, KVR]); ca_kr = din("ca_kr", [PAST, ROPE]); ca_k = din("ca_k", [PAST, HB]); ca_v = din("ca_v", [PAST, HB])
    st_conv = din("st_conv", [2, 2 * DFF])
    y = [dout("y_p", [ROWS, D]), dout("y_s", [DEC, D])]
    o_ckv = [dout("o_ckv_p", [ROWS, KVR]), dout("o_ckv_s", [DEC, KVR])]
    o_kr = [dout("o_kr_p", [ROWS, ROPE]), dout("o_kr_s", [DEC, ROPE])]
    o_k = [dout("o_k_p", [ROWS, HB]), dout("o_k_s", [DEC, HB])]
    o_v = [dout("o_v_p", [ROWS, HB]), dout("o_v_s", [DEC, HB])]
    o_conv = [dout("o_conv_p", [2, 2 * DFF]), dout("o_conv_s", [2, 2 * DFF])]
    SL = [NBUF, SLEN]
    NO = [NOWN, DEC]
    KN = [dscr("KN%d" % j, [8, 64, SL[j]]) for j in range(2)]
    KRT = [dscr("KRT%d" % j, [32, SL[j]]) for j in range(2)]
    VA = [dscr("VA%d" % j, [SL[j], 8 * 128]) for j in range(2)]
    SKT = [dscr("SKT%d" % j, [8, 64, SL[j]]) for j in range(2)]
    SV = [dscr("SV%d" % j, [SL[j], HB]) for j in range(2)]
    HT = [dscr("HT%d" % j, [NCH, 128, NO[j]]) for j in range(2)]
    QLT = [dscr("QLT%d" % j, [3, 128, NO[j]]) for j in range(2)]
    OT = [dscr("OT%d" % j, [16, 64, NO[j]]) for j in range(2)]
    X1 = [dscr("X1%d" % j, [NO[j], D], F32) for j in range(2)]
    MOD = dscr("MOD", [2, 6, D], F32)
    WUPB = dscr("WUPB", [NFC, 128, NCH, 2, 128])

    PROG = Prog()
    semstack = ExitStack()
    PROG.make_sems(nc, semstack)
    C = Ctx(nc, PROG)
    wada = [C.sb("wada%d" % i, [128, NCH, 512]) for i in range(3)]
    wkv = C.sb("wkv", [128, NCH, 1312 + QR], BF16)
    wuk = C.sb("wuk", [128, 2, 512], BF16); wuv = C.sb("wuv", [128, 2, 512], BF16)
    idb = C.sb("idb", [128, 128], BF16)
    cT = C.sb("cT", [128, NCH, 2]); sT = C.sb("sT", [128, NCH, 2])
    bada = C.sb("bada", [128, 48]); adaT = C.sb("adaT", [128, 48, 2])
    gv = C.sb("gv", [128, 4, NCH])
    gq = C.sb("gq", [128, 3])
    modt = C.sb("modt", [128, 2, 6, NCH])
    gkv = C.sb("gkv", [128, KVR])
    pk = C.ps("pk", [128, 512]); pv = C.ps("pv", [128, 512]); pc = C.ps("pc", [128, 512])
    ada_ps = pc
    C.store_q = "pool"
    for t in range(2):
        C.ld(cT[:, :, t], fm(c2[t]), ["cT"], slow=True)
    C.ld(bada[:, :], fm(b_ada), ["bada"], slow=True)
    for i, g in enumerate((g_pre, g_post, g_pre2, g_post2)):
        C.ld(gv[:, i, :], fm(g), ["gv"], slow=True)
    C.ld(gq[:, :], fm(g_q), ["gq"], slow=True)
    C.ld(idb[:, :], ident[:, :], ["idb"])
    C.ld(gkv[:, :], g_kv[0:1, :].partition_broadcast(128), ["gkv"])
    def load_wkv():
        for c in range(NCH):
            rows = slice(c * 128, (c + 1) * 128)
            C.ld(wkv[:, c, 0:288], w_in[rows, C_CKV:C_CKV + 288], ["wkv"], q="pool")
            C.ld(wkv[:, c, 288:1312], w_in[rows, C_SK:C_SK + 1024], ["wkv"], q="pool")
            C.ld(wkv[:, c, 1312:1312 + QR], w_in[rows, C_QL:C_QL + QR], ["wkv"], q="pool")

    for c in range(2):
        C.ld(wuk[:, c, :], w_uk[c * 128:(c + 1) * 128, :], ["wuk"], q="pool")
        C.ld(wuv[:, c, :], w_uv[c * 128:(c + 1) * 128, :], ["wuv"], q="pool")
    def ada_group(g):
        if g == 0:
            C.act(sT[:, :, :], cT[:, :, :], AF.Silu, ["cT"], ["sT"])
        for gg in ([0, 1, 2] if g == 0 else [g + 2]):
            if gg < 12:
                C.ld(wada[gg % 3][:, :, :], w_ada[:, gg * 512:(gg + 1) * 512].rearrange("(c p) n -> p c n", p=128), ["wada%d" % (gg % 3)])
        if True:
            wb = wada[g % 3]; wn = "wada%d" % (g % 3)
            for q4 in range(4):
                for c in range(NCH):
                    C.mm(ada_ps[:, q4 * 2:q4 * 2 + 2], wb[:, c, q4 * 128:(q4 + 1) * 128], sT[:, c, :], c == 0, c == NCH - 1,
                         [wn, "sT"], ["pc"], inc=(c == NCH - 1))
            for t in range(2):
                C.tt(adaT[:, g * 4:(g + 1) * 4, t], ada_ps[:, 0:8].rearrange("p (q t) -> p q t", t=2)[:, :, t], bada[:, g * 4:(g + 1) * 4],
                     ALU.add, ["pc", "bada"], ["adaT"])

    def ada_final():
        for t in range(2):
            C.stt(modt[:, t, 0, :], adaT[:, 8:16, t], 1.0, gv[:, 0, :], ALU.add, ALU.mult, ["adaT", "gv"], ["modt"])
            C.cp(modt[:, t, 1, :], adaT[:, 0:8, t], ["adaT"], ["modt"])
            C.stt(modt[:, t, 2, :], adaT[:, 32:40, t], 1.0, gv[:, 2, :], ALU.add, ALU.mult, ["adaT", "gv"], ["modt"])
            C.cp(modt[:, t, 3, :], adaT[:, 24:32, t], ["adaT"], ["modt"])
            C.tt(modt[:, t, 4, :], adaT[:, 16:24, t], gv[:, 1, :], ALU.mult, ["adaT", "gv"], ["modt"])
            C.tt(modt[:, t, 5, :], adaT[:, 40:48, t], gv[:, 3, :], ALU.mult, ["adaT", "gv"], ["modt"])
            for k in range(6):
                C.store(fm(MOD[t, k]), modt[:, t, k, :], ["modt"], slow=True)


    xt = [C.sb("xt%d" % i, [128, D]) for i in range(2)]
    xn = [C.sb("xn%d" % i, [128, D], BF16) for i in range(2)]
    htmp = C.sb("htmp", [128, NCH, 128])
    junk = C.sb("junk", [128, D], BF16)
    hT = [C.sb("hT%d" % i, [128, NCH, 128], BF16) for i in range(4)]
    cs = [C.sb("cs%d" % i, [128, 32]) for i in range(3)]
    stA = [C.sb("stA%d" % i, [128, 2]) for i in range(2)]
    stB = [C.sb("stB%d" % i, [128, 2]) for i in range(2)]
    stC = [C.sb("stC%d" % i, [128, 2]) for i in range(2)]
    okv = [C.sb("okv%d" % i, [128, 2 * HB]) for i in range(2)]
    ock = [C.sb("ock%d" % i, [128, KVR + ROPE]) for i in range(2)]
    rt = [C.sb("rt%d" % i, [128, 64]) for i in range(2)]
    CB = [C.sb("CB%d" % i, [128, KVR + ROPE], BF16) for i in range(4)]
    KB = [C.sb("KB%d" % i, [128, HB], BF16) for i in range(4)]
    SVB = [C.sb("SVB%d" % i, [128, HB], BF16) for i in range(4)]
    CT = [C.sb("CT%d" % i, [128, 3, 128], BF16) for i in range(2)]
    STt = [C.sb("ST%d" % i, [128, 4, 128], BF16) for i in range(2)]
    KNT = [C.sb("KNT%d" % i, [64, 8, 128], BF16) for i in range(2)]
    VAT = [C.sb("VAT%d" % i, [128, 8, 128], BF16) for i in range(2)]
    QN = [C.sb("QN%d" % i, [128, QR], BF16) for i in range(2)]
    QT = [C.sb("QT%d" % i, [128, 3, 128], BF16) for i in range(2)]
    tp = C.ps("tp", [128, NCH, 128], BF16)
    tpx = C.ps("tpx", [128, 8, 128], BF16)
    tq = C.ps("tq", [128, 8, 128], BF16)
    pkn = C.ps("pkn", [128, 4, 128])
    pva = C.ps("pva", [128, 4, 128])
    for i in range(2):
        C.memset(VAT[i][:, :, 64:128], 1.0, ["VAT%d" % i])

    def rstd2(stat, n, width, rname):
        C.act(stat[0:n, 1:2], stat[0:n, 0:1], AF.Ln, [rname], [rname], scale=1.0 / width, bias=EPS)
        C.act(stat[0:n, 1:2], stat[0:n, 1:2], AF.Exp, [rname], [rname], scale=-0.5)

    def kv_tail(i, job, r0, n, si=None):
        s = "%d" % i
        si = i if si is None else si
        ss = "%d" % si
        for c, (a, b) in enumerate(((0, 128), (128, 256), (256, 288))):
            C.tr(tpx[0:b - a, c, 0:n], CB[si][0:n, a:b], idb[0:n, 0:n], ["CB" + ss, "idb"], ["tpx"], inc=(c == 2))
        for c in range(4):
            C.tr(tpx[:, 3 + c, 0:n], KB[si][0:n, c * 128:(c + 1) * 128], idb[0:n, 0:n], ["KB" + ss, "idb"], ["tpx"], inc=(c == 3))
        C.cp(CT[i][:, 0:2, 0:n], tpx[:, 0:2, 0:n], ["tpx"], ["CT" + s])
        C.cp(CT[i][0:32, 2, 0:n], tpx[0:32, 2, 0:n], ["tpx"], ["CT" + s])
        C.cp(STt[i][:, :, 0:n], tpx[:, 3:7, 0:n], ["tpx"], ["ST" + s])
        C.store(SKT[job].rearrange("(pr two) d r -> (two d) pr r", two=2)[:, :, r0:r0 + n], STt[i][:, :, 0:n], ["ST" + s])
        C.store(KRT[job][:, r0:r0 + n], CT[i][0:32, 2, 0:n], ["CT" + s])
        for (bk, bn, h0) in ((pkn, "pkn", 0), (pva, "pva", 4)):
            for h in range(4):
                for c in range(2):
                    C.mm(bk[0:64, h, 0:n], wuk[:, c, (h0 + h) * 64:(h0 + h + 1) * 64], CT[i][:, c, 0:n], c == 0, c == 1,
                         ["wuk", "CT" + s], [bn], inc=(c == 1 and h == 3))
        C.cp(KNT[i][:, 0:4, 0:n], pkn[0:64, :, 0:n], ["pkn"], ["KNT%s_a" % s], eng="act")
        C.cp(KNT[i][:, 4:8, 0:n], pva[0:64, :, 0:n], ["pva"], ["KNT%s_b" % s])
        C.store(KN[job].rearrange("h d r -> d h r")[:, :, r0:r0 + n], KNT[i][:, :, 0:n], ["KNT%s_a" % s, "KNT%s_b" % s])
        for c in range(2):
            C.mm(pva[0:n, :, :], CT[i][:, c, 0:n], wuv[:, c, :], c == 0, c == 1, ["wuv", "CT" + s], ["pva"], inc=(c == 1))
        C.cp(VAT[i][0:n, :, 0:64], pva[0:n, :, :].rearrange("p a (b e) -> p (a b) e", e=64), ["pva"], ["VAT" + s])
        C.store(VA[job][r0:r0 + n, :], VAT[i][0:n].rearrange("p h e -> p (h e)"), ["VAT" + s])

    tiles = []
    for t in range(NBUF // 128):
        own = None
        for r in range(2):
            t0 = RUN_ROW0[r] // 128
            if t == t0 - 1:
                own = (126, 128, RUN_COL0[r], None)
            elif t0 <= t < t0 + 8:
                own = (0, 128, RUN_COL0[r] + 2 + (t - t0) * 128, r * 1024 + (t - t0) * 128)
        tiles.append((0, xk, t * 128, 128, cs_p, own, t * 128))
    tiles.append((1, x_s, 0, DEC, cs_s, (0, DEC, 0, 0), PAST))
    if ntl is not None:
        tiles = tiles[-ntl:]

    def S0(ti):
        (job, xsrc, r0, n, cssrc, own, kr0) = tiles[ti]
        i2 = "%d" % (ti % 2); i3 = "%d" % (ti % 3); i4 = "%d" % (ti % 4)
        x_ = xt[ti % 2]; xn_ = xn[ti % 2]; st_ = stA[ti % 2]; h_ = hT[ti % 4]
        C.ld(x_[0:n, :], xsrc[r0:r0 + n, :], ["xt" + i2])
        C.ld(cs[ti % 3][0:n, :], cssrc[r0:r0 + n, :], ["cs" + i3])
        C.act(junk[0:n, :], x_[0:n, :], AF.Square, ["xt" + i2], ["junk", "stA" + i2], accum=st_[0:n, 0:1])
        rstd2(st_, n, D, "stA" + i2)
        C.ts(xn_[0:n, :], x_[0:n, :], st_[0:n, 1:2], None, ALU.mult, None, ["xt" + i2, "stA" + i2], ["xn" + i2])
        for c in range(NCH):
            C.tr(tp[:, c, 0:n], xn_[0:n, c * 128:(c + 1) * 128], idb[0:n, 0:n], ["xn" + i2, "idb"], ["tp"], inc=(c == NCH - 1))
        C.tt(htmp[:, :, 0:n], tp[:, :, 0:n], modt[:, job, 0, :].unsqueeze(2).broadcast_to([128, NCH, n]), ALU.mult, ["tp", "modt"], ["htmp"])
        C.tt(h_[:, :, 0:n], htmp[:, :, 0:n], modt[:, job, 1, :].unsqueeze(2).broadcast_to([128, NCH, n]), ALU.add, ["htmp", "modt"], ["hT" + i4])

    def S1(ti):
        (job, xsrc, r0, n, cssrc, own, kr0) = tiles[ti]
        i = ti % 2
        s = "%d" % i; i3 = "%d" % (ti % 3); i4 = "%d" % (ti % 4)
        h_ = hT[ti % 4]; cs_ = cs[ti % 3]; st_ = stB[i]
        for (bank, bn, c0, w) in ((pc, "pc", 0, 288), (pk, "pk", 288, HB), (pv, "pv", 800, HB)):
            for c in range(NCH):
                C.mm(bank[0:n, 0:w], h_[:, c, 0:n], wkv[:, c, c0:c0 + w], c == 0, c == NCH - 1, ["hT" + i4, "wkv"], [bn], inc=(c == NCH - 1))
        outs = own is not None and own[3] is not None
        C.cp(KB[i][0:n, :], pk[0:n, :], ["pk"], ["KB" + s], eng="act")
        C.cp(SVB[i][0:n, :], pv[0:n, :], ["pv"], ["SVB" + s])
        C.store(SV[job][kr0:kr0 + n, :], SVB[i][0:n, :], ["SVB" + s])
        if outs:
            C.cp(okv[i][0:n, 0:HB], pk[0:n, :], ["pk"], ["okv%s_a" % s], eng="act")
            C.cp(okv[i][0:n, HB:2 * HB], pv[0:n, :], ["pv"], ["okv%s_b" % s])
            C.store(o_k[job][own[3]:own[3] + n, :], okv[i][0:n, 0:HB], ["okv%s_a" % s])
            C.store(o_v[job][own[3]:own[3] + n, :], okv[i][0:n, HB:2 * HB], ["okv%s_b" % s])
        C.act(junk[0:n, 0:KVR], pc[0:n, 0:KVR], AF.Square, ["pc"], ["junk", "stB" + s], accum=st_[0:n, 0:1])
        rstd2(st_, n, KVR, "stB" + s)
        C.stt(ock[i][0:n, 0:KVR], pc[0:n, 0:KVR], st_[0:n, 1:2], gkv[0:n, :], ALU.mult, ALU.mult, ["pc", "stB" + s, "gkv"], ["ock" + s])
        x1 = pc[0:n, KVR:KVR + 16]; x2 = pc[0:n, KVR + 16:KVR + 32]
        cosv = cs_[0:n, 0:16]; sinv = cs_[0:n, 16:32]
        C.tt(rt[i][0:n, 0:16], x1, cosv, ALU.mult, ["pc", "cs" + i3], ["rt" + s])
        C.tt(rt[i][0:n, 16:32], x2, sinv, ALU.mult, ["pc", "cs" + i3], ["rt" + s])
        C.tt(rt[i][0:n, 32:48], x2, cosv, ALU.mult, ["pc", "cs" + i3], ["rt" + s])
        C.tt(rt[i][0:n, 48:64], x1, sinv, ALU.mult, ["pc", "cs" + i3], ["rt" + s])
        C.tt(ock[i][0:n, KVR:KVR + 16], rt[i][0:n, 0:16], rt[i][0:n, 16:32], ALU.subtract, ["rt" + s], ["ock" + s])
        C.tt(ock[i][0:n, KVR + 16:KVR + 32], rt[i][0:n, 32:48], rt[i][0:n, 48:64], ALU.add, ["rt" + s], ["ock" + s])
        C.cp(CB[i][0:n, :], ock[i][0:n, :], ["ock" + s], ["CB" + s], eng="pool")
        if outs:
            C.store(o_ckv[job][own[3]:own[3] + n, :], ock[i][0:n, 0:KVR], ["ock" + s])
            C.store(o_kr[job][own[3]:own[3] + n, :], ock[i][0:n, KVR:KVR + ROPE], ["ock" + s])

    def S2(ti):
        (job, xsrc, r0, n, cssrc, own, kr0) = tiles[ti]
        i = ti % 2
        s = "%d" % i; i4 = "%d" % (ti % 4)
        h_ = hT[ti % 4]; st_ = stC[i]
        if alim >= 3:
            kv_tail(i, job, kr0, n)
        if own is not None and alim >= 4:
            lo, hi, col0, _ = own
            w = hi - lo
            for c in range(NCH):
                C.mm(pkn[0:n, 0:3, :], h_[:, c, 0:n], wkv[:, c, 1312:1312 + QR], c == 0, c == NCH - 1, ["hT" + i4, "wkv"], ["pkn"], inc=(c == NCH - 1))
            C.act(junk[0:n, 0:QR], pkn[0:n, 0:3, :], AF.Square, ["pkn"], ["junk", "stC" + s], accum=st_[0:n, 0:1])
            rstd2(st_, n, QR, "stC" + s)
            C.ts(QN[i][0:n, :], pkn[0:n, 0:3, :], st_[0:n, 1:2], None, ALU.mult, None, ["pkn", "stC" + s], ["QN" + s])
            for c in range(3):
                C.tr(tq[:, c, 0:n], QN[i][0:n, c * 128:(c + 1) * 128], idb[0:n, 0:n], ["QN" + s, "idb"], ["tq"], inc=(c == 2))
            for c in range(3):
                C.ts(QT[i][:, c, 0:n], tq[:, c, 0:n], gq[:, c:c + 1], None, ALU.mult, None, ["tq", "gq"], ["QT" + s])
            C.store(QLT[job].rearrange("c p n -> p c n")[:, :, col0:col0 + w], QT[i][:, :, lo:hi], ["QT" + s])
            C.store(HT[job].rearrange("c p n -> p c n")[:, :, col0:col0 + w], h_[:, :, lo:hi], ["hT" + i4])

    C.store_q = "sp"
    ada_done = set()
    NCT = PAST // 128 if alim >= 5 else 0

    def cache_loads(t):
        si = t % 4
        ss = "%d" % si
        r0 = t * 128
        C.ld(CB[si][:, 0:KVR], ca_ckv[r0:r0 + 128, :], ["CB" + ss], q="pool")
        C.ld(CB[si][:, KVR:KVR + ROPE], ca_kr[r0:r0 + 128, :], ["CB" + ss], q="pool")
        C.ld(KB[si][:, :], ca_k[r0:r0 + 128, :], ["KB" + ss], q="pool")
        C.ld(SVB[si][:, :], ca_v[r0:r0 + 128, :], ["SVB" + ss], q="pool")

    for t in range(min(2, NCT)):
        cache_loads(t)
    for t in range(NCT):
        if t + 2 < NCT:
            cache_loads(t + 2)
        si = t % 4
        r0 = t * 128
        C.store(SV[1][r0:r0 + 128, :], SVB[si][:, :], ["SVB%d" % si])
        kv_tail(t % 2, 1, r0, 128, si=si)
        if t == 3:
            load_wkv()
        if t % 2 == 1 and t // 2 < 12:
            ada_group(t // 2)
            ada_done.add(t // 2)
    if (PAST // 128 if alim >= 5 else 0) <= 3:
        load_wkv()
    C.store_q = "pool"
    for g in range(12):
        if g not in ada_done:
            ada_group(g)
    ada_final()
    NTL = len(tiles)
    for step in range(NTL + 2):
        if step < NTL:
            S0(step)
        if 0 <= step - 1 < NTL:
            S1(step - 1)
        if 0 <= step - 2 < NTL:
            S2(step - 2)
    C.finish()
    if nphase == 1:
        return nc

    C = Ctx(nc, PROG)
    HGMAX = 8
    wsq = C.sb("wsq", [128, NCH, HB], BF16)
    wuq = C.sb("wuq", [128, 3, 768], BF16)
    wur = C.sb("wur", [128, 3, 8, 96], BF16)
    mtri = C.sb("mtri", [128, 128], BF16); negu = C.sb("negu", [128, 128], BF16)
    negone = C.sb("negone", [128, 128], BF16); zl = C.sb("zl", [128, 128], BF16); zr = C.sb("zr", [128, 512], BF16)
    hq = C.sb("hq", [128, NCH, 512], BF16); ql = C.sb("ql", [128, 3, 512], BF16)
    cst = C.sb("cst", [96, 512]); snt = C.sb("snt", [96, 512])
    qcat = C.sb("qcat", [97, 8, 512], BF16); sbq = C.sb("sbq", [65, 8, 512], BF16)
    t1 = C.sb("t1", [96, 512]); t2 = C.sb("t2", [96, 512])
    kc = [C.sb("kc%d" % i, [97, HGMAX, 512], BF16) for i in range(3)]
    va = [C.sb("va%d" % i, [128, 4, HGMAX, 128], BF16) for i in range(3)]
    sk = [C.sb("sk%d" % i, [65, HGMAX, 512], BF16) for i in range(3)]
    sv = [C.sb("sv%d" % i, [128, 4, HGMAX, 64], BF16) for i in range(3)]
    Pb = [C.sb("Pb%d" % i, [128, 512], BF16) for i in range(4)]
    Eb = [C.sb("Eb%d" % i, [128, 512]) for i in range(3)]
    SPb = [C.sb("SPb%d" % i, [128, 512], BF16) for i in range(3)]
    Ab = [C.sb("Ab%d" % i, [128, 512], BF16) for i in range(2)]
    Xb = [C.sb("Xb%d" % i, [128, 512]) for i in range(2)]
    Rb = C.sb("Rb", [128, 4, 512], BF16)
    Rs = C.sb("Rs", [128, 8, DEC], BF16)
    rc = C.sb("rc", [64, 512]); ot = [C.sb("ot%d" % i, [64, 512], BF16) for i in range(2)]
    bank = [C.ps("bank%d" % i, [128, 512]) for i in range(8)]
    for c in range(NCH):
        C.ld(wsq[:, c, :], w_in[c * 128:(c + 1) * 128, C_SQ:C_SQ + HB], ["wsq"], q="pool")
    C.memset(wur[:, :, :, :], 0.0, ["wur"])
    for c in range(3):
        C.ld(wuq[:, c, :], w_uq[c * 128:(c + 1) * 128, :], ["wuq"], q="pool")
        src = w_uq[c * 128:(c + 1) * 128, :].rearrange("p (h e) -> p h e", e=96)
        C.ld(wur[:, c, :, 64:80], src[:, :, 80:96], ["wur"], q="pool")
        C.ld(wur[:, c, :, 80:96], src[:, :, 64:80], ["wur"], q="pool")
    C.ts(wur[:, :, :, 64:80], wur[:, :, :, 64:80], -1.0, None, ALU.mult, None, ["wur"], ["wur"])
    C.ld(mtri[:, :], mtri_d[:, :], ["mtri"]); C.ld(negu[:, :], negu_d[:, :], ["negu"])
    C.memset(negone[:, :], -1.0, ["negone"]); C.memset(zl[:, :], 0.0, ["zl"]); C.memset(zr[:, :], 0.0, ["zr"])
    C.ld(qcat[96:97, :, :], ones_d[0:1, :].rearrange("o (h n) -> o h n", h=8), ["qcat"])
    C.ld(sbq[64:65, :, :], ones_d[0:1, :].rearrange("o (h n) -> o h n", h=8), ["sbq"])

    for job in range(2):
        for (col0, nq, nchunk, kind) in qtiles(job):
            HG = 4 if kind == "big" else 8
            GB = 1 if kind == "big" else HG
            Rsel = Rb if kind == "big" else Rs
            C.ld(hq[:, :, 0:nq], HT[job].rearrange("c p n -> p c n")[:, :, col0:col0 + nq], ["hq"])
            C.ld(ql[:, :, 0:nq], QLT[job].rearrange("c p n -> p c n")[:, :, col0:col0 + nq], ["ql"])
            C.ld(cst[64:96, 0:nq], cq[job][:, col0:col0 + nq], ["cst"])
            C.ld(snt[64:96, 0:nq], sq[job][:, col0:col0 + nq], ["snt"])
            for h in range(8):
                pa = bank[(2 * h) % 8]; pb = bank[(2 * h + 1) % 8]
                na = "bank%d" % ((2 * h) % 8); nb_ = "bank%d" % ((2 * h + 1) % 8)
                for c in range(3):
                    C.mm(pa[0:96, 0:nq], wuq[:, c, h * 96:(h + 1) * 96], ql[:, c, 0:nq], c == 0, c == 2, ["wuq", "ql"], [na], inc=(c == 2))
                for c in range(3):
                    C.mm(pb[0:96, 0:nq], wur[:, c, h, :], ql[:, c, 0:nq], c == 0, c == 2, ["wur", "ql"], [nb_], inc=(c == 2))
                C.cp(qcat[0:64, h, 0:nq], pa[0:64, 0:nq], [na], ["qcat"], eng="act")
                C.tt(t1[64:96, 0:nq], pa[64:96, 0:nq], cst[64:96, 0:nq], ALU.mult, [na, "cst"], ["t1"])
                C.tt(t2[64:96, 0:nq], pb[64:96, 0:nq], snt[64:96, 0:nq], ALU.mult, [nb_, "snt"], ["t2"])
                C.tt(qcat[64:96, h, 0:nq], t1[64:96, 0:nq], t2[64:96, 0:nq], ALU.add, ["t1", "t2"], ["qcat"])
            for h in range(8):
                pa = bank[h]; na = "bank%d" % h
                for c in range(NCH):
                    C.mm(pa[0:64, 0:nq], wsq[:, c, h * 64:(h + 1) * 64], hq[:, c, 0:nq], c == 0, c == NCH - 1, ["wsq", "hq"], [na], inc=(c == NCH - 1))
                C.act(sbq[0:64, h, 0:nq], pa[0:64, 0:nq], AF.Identity, [na], ["sbq"], scale=0.125)

            def blocks():
                out = []
                for ch in range(nchunk - 1, -1, -1):
                    if kind == "samp" and ch == nchunk - 1:
                        out.append((ch, 0, DEC, 0, "samp"))
                        continue
                    for kb in range(3, -1, -1):
                        if kind == "big" and ch == nchunk - 1:
                            out.append((ch, kb, 128, 128 * kb, "diag"))
                        elif kind == "halo" and ch == nchunk - 1 and kb == 3:
                            out.append((ch, kb, 128, 0, "halo"))
                        else:
                            out.append((ch, kb, 128, 0, None))
                return out

            blist = blocks()
            obank = lambda hh: (bank[4 + (hh * nq) // 512], "bank%d" % (4 + (hh * nq) // 512), (hh * nq) % 512)
            items0 = []
            for (ch, kb, nk, qa, diag) in blist:
                for b0 in range(0, HG, GB):
                    items0.append((ch, kb, nk, qa, diag, b0))
            chunks_desc = []
            for it_ in items0:
                if not chunks_desc or chunks_desc[-1] != it_[0]:
                    chunks_desc.append(it_[0])

            def load_chunk(att, hg, ch, li):
                k0 = ch * 512
                kw = 512 if not (kind == "samp" and ch == nchunk - 1) else DEC
                hs = slice(hg * HG, (hg + 1) * HG)
                if att == "mla":
                    C.ld(kc[li][0:64, 0:HG, 0:kw], KN[job].rearrange("h d r -> d h r")[:, hs, k0:k0 + kw], ["kc%d_a" % li])
                    C.ld(kc[li][64:96, 0:HG, 0:kw], KRT[job][:, k0:k0 + kw].unsqueeze(1).broadcast_to([32, HG, kw]), ["kc%d_b" % li])
                    C.ld(kc[li][96:97, 0:HG, 0:kw], kbias[job][0:1, k0:k0 + kw].unsqueeze(1).broadcast_to([1, HG, kw]), ["kc%d_c" % li])
                    if kw == 512:
                        C.ld(va[li][:, :, 0:HG, :], VA[job][k0:k0 + 512, :].rearrange("(b p) (h e) -> p b h e", p=128, e=128)[:, :, hs, :], ["va%d" % li])
                    else:
                        C.ld(va[li][0:kw, 0, 0:HG, :], VA[job][k0:k0 + kw, :].rearrange("p (h e) -> p h e", e=128)[:, hs, :], ["va%d" % li])
                else:
                    C.ld(sk[li][0:64, 0:HG, 0:kw], SKT[job].rearrange("h d r -> d h r")[:, hs, k0:k0 + kw], ["sk%d_a" % li])
                    C.ld(sk[li][64:65, 0:HG, 0:kw], kbias[job][0:1, k0:k0 + kw].unsqueeze(1).broadcast_to([1, HG, kw]), ["sk%d_b" % li])
                    if kw == 512:
                        C.ld(sv[li][:, :, 0:HG, :], SV[job][k0:k0 + 512, :].rearrange("(b p) (h e) -> p b h e", p=128, e=64)[:, :, hs, :], ["sv%d" % li])
                    else:
                        C.ld(sv[li][0:kw, 0, 0:HG, :], SV[job][k0:k0 + kw, :].rearrange("p (h e) -> p h e", e=64)[:, hs, :], ["sv%d" % li])

            ldc = 0
            for att in ("mla", "sb"):
                for hg in range(8 // HG):
                    nob = max(1, (HG * nq) // 512)
                    for ob in range(nob):
                        C.mm(bank[4 + ob][:, 0:min(512, HG * nq)], zl[:, :], zr[:, 0:min(512, HG * nq)], True, False, ["zl", "zr"], ["bank%d" % (4 + ob)])
                    if att == "sb":
                        for hh in range(HG):
                            C.memset(Rsel[:, hh, :], 0.0, ["Rb%d" % hh])
                    cbuf = {}
                    for ci, ch in enumerate(chunks_desc):
                        cbuf[ch] = (ldc + ci) % 3
                    ldc += len(chunks_desc)
                    loaded = set()

                    def ensure(ch):
                        if ch not in loaded:
                            loaded.add(ch)
                            load_chunk(att, hg, ch, cbuf[ch])

                    NI = len(items0)
                    first_blk = (items0[0][0], items0[0][1])

                    def geom(ii):
                        (ch, kb, nk, qa, diag, b0) = items0[ii]
                        heads = list(range(b0, b0 + GB))
                        if GB == 1:
                            coff = {b0: 0}
                            c_lo, c_hi = qa, nq
                        else:
                            coff = {hh: (hh - b0) * nq for hh in heads}
                            c_lo, c_hi = 0, GB * nq
                        return ch, kb, nk, qa, diag, heads, coff, c_lo, c_hi, cbuf[ch], slice(kb * 128, kb * 128 + nk)

                    def msk(buf, bname, diag, heads, coff, qa):
                        if diag == "diag":
                            C.tt(buf[:, qa:qa + 128], buf[:, qa:qa + 128], mtri[:, :], ALU.mult, [bname, "mtri"], [bname])
                        elif diag == "halo":
                            for hh in heads:
                                C.tt(buf[:, coff[hh]:coff[hh] + 2], buf[:, coff[hh]:coff[hh] + 2], mtri[:, 126:128], ALU.mult, [bname, "mtri"], [bname])
                        elif diag == "samp":
                            for hh in heads:
                                C.tt(buf[0:DEC, coff[hh]:coff[hh] + DEC], buf[0:DEC, coff[hh]:coff[hh] + DEC], mtri[0:DEC, 0:DEC], ALU.mult, [bname, "mtri"], [bname])

                    def s1(ii):
                        ch, kb, nk, qa, diag, heads, coff, c_lo, c_hi, li, ks = geom(ii)
                        ensure(ch)
                        cpos = chunks_desc.index(ch)
                        if cpos + 1 < len(chunks_desc):
                            ensure(chunks_desc[cpos + 1])
                        if att == "mla":
                            sb_ = bank[ii % 4]; sn = "bank%d" % (ii % 4)
                            pb_ = Pb[ii % 4]; pn = "Pb%d" % (ii % 4)
                            for hh in heads:
                                C.mm(sb_[0:nk, coff[hh] + qa:coff[hh] + nq], kc[li][0:97, hh, ks], qcat[0:97, hg * HG + hh, qa:nq], True, True,
                                     ["kc%d_a" % li, "kc%d_b" % li, "kc%d_c" % li, "qcat"], [sn])
                            C.act(pb_[0:nk, c_lo:c_hi], sb_[0:nk, c_lo:c_hi], AF.Exp, [sn], [pn], scale=MLA_SCALE)
                            if diag == "diag":
                                C.memset(pb_[64:128, qa:qa + 64], 0.0, [pn])
                        else:
                            zb_ = bank[ii % 2]; zn = "bank%d" % (ii % 2)
                            eb_ = Eb[ii % 3]; en = "Eb%d" % (ii % 3)
                            sp_ = SPb[ii % 3]; spn = "SPb%d" % (ii % 3)
                            for hh in heads:
                                C.mm(zb_[0:nk, coff[hh] + qa:coff[hh] + nq], sk[li][0:65, hh, ks], sbq[0:65, hg * HG + hh, qa:nq], True, True,
                                     ["sk%d_a" % li, "sk%d_b" % li, "sbq"], [zn])
                            C.act(eb_[0:nk, c_lo:c_hi], zb_[0:nk, c_lo:c_hi], AF.Exp, [zn], [en])
                            if not SB5:
                                msk(eb_, en, diag, heads, coff, qa)
                            C.act(sp_[0:nk, c_lo:c_hi], eb_[0:nk, c_lo:c_hi], AF.Ln, [en], [spn], bias=1.0)
                            if SB5:
                                msk(sp_, spn, diag, heads, coff, qa)

                    def s2(ii):
                        ch, kb, nk, qa, diag, heads, coff, c_lo, c_hi, li, ks = geom(ii)
                        if att == "mla":
                            pb_ = Pb[ii % 4]; pn = "Pb%d" % (ii % 4)
                            for hh in heads:
                                ob, on, oo = obank(hh)
                                C.mm(ob[:, oo + qa:oo + nq], va[li][0:nk, kb, hh, :], pb_[0:nk, coff[hh] + qa:coff[hh] + nq], False, False,
                                     ["va%d" % li, pn], [on])
                            return
                        first = (ch, kb) == first_blk
                        ab_ = bank[2 + ii % 2]; an = "bank%d" % (2 + ii % 2)
                        sp_ = SPb[ii % 3]; spn = "SPb%d" % (ii % 3)
                        a_ = Ab[ii % 2]; abn = "Ab%d" % (ii % 2)
                        eb_ = Eb[ii % 3]; en = "Eb%d" % (ii % 3)
                        xb_ = Xb[ii % 2]; xn_ = "Xb%d" % (ii % 2)
                        for hh in heads:
                            cs_ = slice(coff[hh] + qa, coff[hh] + nq)
                            if SB5:
                                C.mm(ab_[0:nk, cs_], sk[li][0:65, hh, ks], sbq[0:65, hg * HG + hh, qa:nq], True, False,
                                     ["sk%d_a" % li, "sk%d_b" % li, "sbq"], [an], inc=False)
                            C.mm(ab_[0:nk, cs_], negu[0:nk, 0:nk], sp_[0:nk, cs_], not SB5, first, ["negu", spn], [an], inc=first)
                            if not first:
                                C.mm(ab_[0:nk, cs_], negone[:, 0:nk], Rsel[:, hh, qa:nq], False, True, ["negone", "Rb%d" % hh], [an])
                        if GB == 1:
                            hh = heads[0]
                            C.tt(Rsel[0:nk, hh, qa:nq], Rsel[0:nk, hh, qa:nq], sp_[0:nk, qa:nq], ALU.add, ["Rb%d" % hh, spn], ["Rb%d" % hh], eng="pool")
                        else:
                            rn = ["Rb%d" % hh for hh in heads]
                            C.tt(Rsel[0:nk, :, 0:nq], Rsel[0:nk, :, 0:nq], sp_[0:nk, 0:GB * nq].rearrange("p (h n) -> p h n", n=nq), ALU.add,
                                 rn + [spn], rn, eng="pool")
                        if SB5:
                            C.act(a_[0:nk, c_lo:c_hi], ab_[0:nk, c_lo:c_hi], AF.Exp, [an], [abn])
                            msk(a_, abn, diag, heads, coff, qa)
                        else:
                            C.act(xb_[0:nk, c_lo:c_hi], ab_[0:nk, c_lo:c_hi], AF.Exp, [an], [xn_])
                            C.tt(a_[0:nk, c_lo:c_hi], eb_[0:nk, c_lo:c_hi], xb_[0:nk, c_lo:c_hi], ALU.mult, [en, xn_], [abn])

                    def s3(ii):
                        ch, kb, nk, qa, diag, heads, coff, c_lo, c_hi, li, ks = geom(ii)
                        a_ = Ab[ii % 2]; abn = "Ab%d" % (ii % 2)
                        for hh in heads:
                            ob, on, oo = obank(hh)
                            C.mm(ob[0:64, oo + qa:oo + nq], sv[li][0:nk, kb, hh, :], a_[0:nk, coff[hh] + qa:coff[hh] + nq], False, False,
                                 ["sv%d" % li, abn], [on])

                    if att == "mla":
                        LAG = 2
                        for step in range(NI + LAG):
                            if step < NI:
                                s1(step)
                            if 0 <= step - LAG < NI:
                                s2(step - LAG)
                    else:
                        for step in range(NI + 2):
                            if step < NI:
                                s1(step)
                            if 0 <= step - 1 < NI:
                                s2(step - 1)
                            if 0 <= step - 2 < NI:
                                s3(step - 2)
                    for hh in range(HG):
                        ob, on, oo = obank(hh)
                        oi = hh % 2
                        if att == "mla":
                            C.ts(rc[:, 0:nq], ob[64:128, oo:oo + nq], 1e-30, None, ALU.max, None, [on], ["rc"])
                            C.recip(rc[:, 0:nq], rc[:, 0:nq], ["rc"], ["rc"])
                            C.tt(ot[oi][:, 0:nq], ob[0:64, oo:oo + nq], rc[:, 0:nq], ALU.mult, [on, "rc"], ["ot%d" % oi])
                            hidx = hg * HG + hh
                        else:
                            C.cp(ot[oi][:, 0:nq], ob[0:64, oo:oo + nq], [on], ["ot%d" % oi], eng="act")
                            hidx = 8 + hg * HG + hh
                        C.store(OT[job][hidx, :, col0:col0 + nq], ot[oi][:, 0:nq], ["ot%d" % oi])
    C.finish()
    if nphase == 2:
        return nc

    C = Ctx(nc, PROG)
    wg = C.sb("wg", [128, NCH, 2048], BF16)
    wpa = C.sb("wpa", [128, 4, D], BF16); wpb = C.sb("wpb", [128, 4, D], BF16)
    wo = C.sb("wo", [128, NCH, D], BF16)
    gt1 = [C.sb("gt1_%d" % j, [128, D]) for j in range(2)]
    hq = C.sb("hq", [128, NCH, 512], BF16); oq = C.sb("oq", [128, 8, 512], BF16)
    sig = C.sb("sig", [128, 16, 512], BF16); mg = C.sb("mg", [128, NCH, 512], BF16)
    ta = [C.sb("ta%d" % i, [128, 512]) for i in range(2)]; tb = [C.sb("tb%d" % i, [128, 512]) for i in range(2)]
    xo = [C.sb("xo%d" % i, [128, D]) for i in range(2)]
    mo = [C.sb("mo%d" % i, [128, D]) for i in range(2)]
    stat = [C.sb("stat%d" % i, [128, 8]) for i in range(2)]
    junk = C.sb("junk", [128, D], BF16)
    bank = [C.ps("bank%d" % i, [128, 512]) for i in range(8)]
    for c in range(NCH):
        C.ld(wg[:, c, :], w_in[c * 128:(c + 1) * 128, C_GA:C_GA + 2048], ["wg"], q="pool")
        C.ld(wo[:, c, :], w_out[c * 128:(c + 1) * 128, :], ["wo"], q="pool")
        if c < 4:
            C.ld(wpa[:, c, :], w_pa[c * 128:(c + 1) * 128, :], ["wpa"], q="pool")
            C.ld(wpb[:, c, :], w_pb[c * 128:(c + 1) * 128, :], ["wpb"], q="pool")
    for j in range(2):
        C.ld(gt1[j][:, :], MOD[j, 4:5, :].partition_broadcast(128), ["gt1"])
    for c in range(NCH):
        for ab in range(2):
            C.ld(WUPB[:, :, c, ab, :].rearrange("pr p n -> p pr n"),
                 w_up[c * 128:(c + 1) * 128, ab * DFF:(ab + 1) * DFF].rearrange("p (pr n) -> p pr n", n=128), ["wupb%d_%d" % (c, ab)], q="pool")
    it = 0
    for job in range(2):
        for (col0, nq, nchunk, kind) in qtiles(job):
            C.ld(hq[:, :, 0:nq], HT[job].rearrange("c p n -> p c n")[:, :, col0:col0 + nq], ["hq"])
            C.ld(oq[:, :, 0:nq], OT[job].rearrange("(pr two) d n -> (two d) pr n", two=2)[:, :, col0:col0 + nq], ["oq"])
            for cc in range(16):
                pb_ = bank[cc % 2]; pn = "bank%d" % (cc % 2)
                for c in range(NCH):
                    C.mm(pb_[:, 0:nq], wg[:, c, cc * 128:(cc + 1) * 128], hq[:, c, 0:nq], c == 0, c == NCH - 1, ["wg", "hq"], [pn], inc=(c == NCH - 1))
                C.act(sig[:, cc, 0:nq], pb_[:, 0:nq], AF.Sigmoid, [pn], ["sig"])
            for cc in range(NCH):
                i2 = cc % 2
                pa_ = bank[2 + 2 * i2]; pan = "bank%d" % (2 + 2 * i2)
                pb_ = bank[3 + 2 * i2]; pbn = "bank%d" % (3 + 2 * i2)
                for h in range(4):
                    C.mm(pa_[:, 0:nq], wpa[:, h, cc * 128:(cc + 1) * 128], oq[:, h, 0:nq], h == 0, h == 3, ["wpa", "oq"], [pan], inc=(h == 3))
                for h in range(4):
                    C.mm(pb_[:, 0:nq], wpb[:, h, cc * 128:(cc + 1) * 128], oq[:, 4 + h, 0:nq], h == 0, h == 3, ["wpb", "oq"], [pbn], inc=(h == 3))
                C.tt(ta[i2][:, 0:nq], pa_[:, 0:nq], sig[:, cc, 0:nq], ALU.mult, [pan, "sig"], ["ta%d" % i2])
                C.tt(tb[i2][:, 0:nq], pb_[:, 0:nq], sig[:, 8 + cc, 0:nq], ALU.mult, [pbn, "sig"], ["tb%d" % i2])
                C.tt(mg[:, cc, 0:nq], ta[i2][:, 0:nq], tb[i2][:, 0:nq], ALU.add, ["ta%d" % i2, "tb%d" % i2], ["mg"])
            for s0 in range(0, nq, 128):
                ns = min(128, nq - s0)
                i2 = it % 2
                it += 1
                s = "%d" % i2
                xrow = own_bufrow(col0 + s0) if job == 0 else s0
                xsrc = xk if job == 0 else x_s
                C.ld(xo[i2][0:ns, :], xsrc[xrow:xrow + ns, :], ["xo" + s])
                for half in range(2):
                    pb_ = bank[6 + half]; pn = "bank%d" % (6 + half)
                    for cc in range(NCH):
                        C.mm(pb_[0:ns, :], mg[:, cc, s0:s0 + ns], wo[:, cc, half * 512:(half + 1) * 512], cc == 0, cc == NCH - 1, ["mg", "wo"], [pn], inc=(cc == NCH - 1))
                    C.act(junk[0:ns, 0:512], pb_[0:ns, :], AF.Square, [pn], ["junk", "stat" + s], accum=stat[i2][0:ns, half:half + 1])
                    C.tt(mo[i2][0:ns, half * 512:(half + 1) * 512], pb_[0:ns, :], gt1[job][0:ns, half * 512:(half + 1) * 512], ALU.mult, [pn, "gt1"], ["mo" + s])
                C.tt(stat[i2][0:ns, 2:3], stat[i2][0:ns, 0:1], stat[i2][0:ns, 1:2], ALU.add, ["stat" + s], ["stat" + s])
                C.rstd(stat[i2], ns, 2, D, "stat" + s)
                C.stt(xo[i2][0:ns, :], mo[i2][0:ns, :], stat[i2][0:ns, 3:4], xo[i2][0:ns, :], ALU.mult, ALU.add, ["mo" + s, "stat" + s, "xo" + s], ["xo" + s])
                C.store(X1[job][col0 + s0:col0 + s0 + ns, :], xo[i2][0:ns, :], ["xo" + s])
    C.finish()
    if nphase == 3:
        return nc

    C = Ctx(nc, PROG)
    wdn = C.sb("wdn", [128, NFC, D], BF16)
    wu = [C.sb("wu%d" % i, [128, NCH, 2, 128], BF16) for i in range(3)]
    idb = C.sb("idb", [128, 128], BF16)
    gt2 = [C.sb("gt2_%d" % j, [128, D]) for j in range(2)]
    m2 = C.sb("m2", [128, 2, 2, NCH])
    cw = C.sb("cw", [128, 3, 44]); cb = C.sb("cb", [128, 44]); hv = C.sb("hv", [128, 2])
    UC = C.sb("UC", [128, 44, 2])
    h2 = [C.sb("h2_%d" % i, [128, NCH, 512], BF16) for i in range(2)]
    G = C.sb("G", [128, NFC, 512], BF16)
    x1t = [C.sb("x1t%d" % i, [128, D]) for i in range(2)]
    xn = [C.sb("xn%d" % i, [128, D], BF16) for i in range(2)]
    mo = [C.sb("mo%d" % i, [128, D]) for i in range(2)]
    stat = [C.sb("stat%d" % i, [128, 8]) for i in range(2)]
    junk = C.sb("junk", [128, D], BF16)
    Ua = [C.sb("Ua%d" % i, [128, 516]) for i in range(2)]; Ub = [C.sb("Ub%d" % i, [128, 516]) for i in range(2)]
    Ta = [C.sb("Ta%d" % i, [128, 512]) for i in range(3)]; Tb = [C.sb("Tb%d" % i, [128, 512]) for i in range(3)]
    W1 = [C.sb("W1%d" % i, [128, 512]) for i in range(2)]; W2 = [C.sb("W2%d" % i, [128, 512]) for i in range(2)]
    tp = C.ps("tp", [128, NCH, 128], BF16)
    pua = [C.ps("pua%d" % i, [128, 512]) for i in range(2)]; pub = [C.ps("pub%d" % i, [128, 512]) for i in range(2)]
    pf = [C.ps("pf%d" % i, [128, 512]) for i in range(2)]
    for c in range(NFC):
        C.ld(wdn[:, c, :], w_dn[c * 128:(c + 1) * 128, :], ["wdn"], q="pool")
    C.ld(idb[:, :], ident[:, :], ["idb"])
    for j in range(2):
        C.ld(gt2[j][:, :], MOD[j, 5:6, :].partition_broadcast(128), ["gt2"])
        C.ld(m2[:, j, 0, :], fm(MOD[j, 2]), ["m2"], slow=True)
        C.ld(m2[:, j, 1, :], fm(MOD[j, 3]), ["m2"], slow=True)
    for k in range(3):
        C.ld(cw[:, k, :], fm(conv_w[k]), ["cw"], slow=True)
    C.ld(cb[:, :], fm(conv_b), ["cb"], slow=True)
    C.ld(hv[:, :], hv_d[0:1, :].partition_broadcast(128), ["hv"])
    UCALL = ["UC%d" % p for p in range(NFC)]
    C.memset(UC[:, :, :], 0.0, UCALL)
    itc = [0]
    wbase = 0
    alltiles = [(job,) + t for job in range(2) for t in qtiles(job)]

    def h2_stage(k):
        (job, col0, nq, nchunk, kind) = alltiles[k]
        hb = h2[k % 2]; hn = "h2_%d" % (k % 2)
        for s0 in range(0, nq, 128):
            ns = min(128, nq - s0)
            i2 = itc[0] % 2
            itc[0] += 1
            s = "%d" % i2
            C.ld(x1t[i2][0:ns, :], X1[job][col0 + s0:col0 + s0 + ns, :], ["x1t" + s])
            C.act(junk[0:ns, :], x1t[i2][0:ns, :], AF.Square, ["x1t" + s], ["junk", "stat" + s], accum=stat[i2][0:ns, 0:1])
            C.rstd(stat[i2], ns, 0, D, "stat" + s)
            C.ts(xn[i2][0:ns, :], x1t[i2][0:ns, :], stat[i2][0:ns, 1:2], None, ALU.mult, None, ["x1t" + s, "stat" + s], ["xn" + s])
            for c in range(NCH):
                C.tr(tp[:, c, 0:ns], xn[i2][0:ns, c * 128:(c + 1) * 128], idb[0:ns, 0:ns], ["xn" + s, "idb"], ["tp"], inc=(c == NCH - 1))
            for c in range(NCH):
                if c % 2 == 0:
                    C.act(hb[:, c, s0:s0 + ns], tp[:, c, 0:ns], AF.Identity, ["tp", "m2"], [hn], scale=m2[:, job, 0, c:c + 1], bias=m2[:, job, 1, c:c + 1])
                else:
                    C.ts(hb[:, c, s0:s0 + ns], tp[:, c, 0:ns], m2[:, job, 0, c:c + 1], m2[:, job, 1, c:c + 1], ALU.mult, ALU.add, ["tp", "m2"], [hn])

    h2_stage(0)
    for k, (job, col0, nq, nchunk, kind) in enumerate(alltiles):
        if True:
            if job == 1 and alltiles[k - 1][0] == 0:
                for t in range(2):
                    C.ld(UC[:, :, t], fm(st_conv[t]), UCALL, slow=True)
            h2b = h2[k % 2]; h2n = "h2_%d" % (k % 2)
            hrun = 1 if (job == 0 and col0 >= RUN_COL0[1]) else 0
            ucn = lambda p: "UC%d" % p

            def ldw(p):
                w3 = (wbase + p) % 3
                C.ld(wu[w3][:, :, :, :], WUPB[p], ["wu%d" % w3])

            def t1(p):
                w3 = (wbase + p) % 3
                i2 = p % 2
                s = "%d" % i2
                wn = "wu%d" % w3
                if p + 1 < NFC:
                    ldw(p + 1)
                for (ab, pu, pun, U, un, fc) in ((0, pua[i2], "pua" + s, Ua[i2], "Ua" + s, p), (1, pub[i2], "pub" + s, Ub[i2], "Ub" + s, NFC + p)):
                    for c in range(NCH):
                        C.mm(pu[:, 0:nq], wu[w3][:, c, ab, :], h2b[:, c, 0:nq], c == 0, c == NCH - 1, [wn, h2n], [pun], inc=(c == NCH - 1))
                    C.act(U[:, 0:2], UC[:, fc, :], AF.Copy, [ucn(p)], [un + "c"])
                    C.act(U[:, 2:2 + nq], pu[:, 0:nq], AF.Copy, [pun], [un])
                    if kind == "halo":
                        C.act(UC[:, fc, :], U[:, 2:4], AF.Copy, [un, "hv"], [ucn(p)], scale=hv[:, hrun:hrun + 1])
                    else:
                        C.act(UC[:, fc, :], U[:, nq:nq + 2], AF.Copy, [un, un + "c"], [ucn(p)])

            def t2(p):
                i2 = p % 2
                s = "%d" % i2
                i3 = "%d" % (p % 3)
                for (U, un, T, tn, fc) in ((Ua[i2], "Ua" + s, Ta[p % 3], "Ta" + i3, p), (Ub[i2], "Ub" + s, Tb[p % 3], "Tb" + i3, NFC + p)):
                    C.ts(T[:, 0:nq], U[:, 2:2 + nq], cw[:, 2, fc:fc + 1], cb[:, fc:fc + 1], ALU.mult, ALU.add, [un, "cw", "cb"], [tn], eng="pool")
                    C.stt(T[:, 0:nq], U[:, 1:1 + nq], cw[:, 1, fc:fc + 1], T[:, 0:nq], ALU.mult, ALU.add, [un, un + "c", "cw", tn], [tn])
                    C.stt(T[:, 0:nq], U[:, 0:nq], cw[:, 0, fc:fc + 1], T[:, 0:nq], ALU.mult, ALU.add, [un, un + "c", "cw", tn], [tn])

            def t4(p):
                i3 = "%d" % (p % 3)
                s = "%d" % (p % 2)
                A_ = Ta[p % 3]; B_ = Tb[p % 3]; w2_ = W2[p % 2]
                C.act(w2_[:, 0:nq], A_[:, 0:nq], AF.Gelu_apprx_tanh, ["Ta" + i3], ["W2" + s])
                C.tt(G[:, p, 0:nq], w2_[:, 0:nq], B_[:, 0:nq], ALU.mult, ["W2" + s, "Tb" + i3], ["G"])

            ldw(0)
            for st_ in range(NFC + 3):
                if kind != "halo":
                    if 0 <= st_ - 1 < NFC:
                        t2(st_ - 1)
                    if 0 <= st_ - 2 < NFC:
                        t4(st_ - 2)
                if st_ < NFC:
                    t1(st_)
            wbase += NFC
            if k + 1 < len(alltiles):
                h2_stage(k + 1)
            last_of_job = (k + 1 == len(alltiles)) or (alltiles[k + 1][0] != job)
            for s0 in (range(0, nq, 128) if kind != "halo" else []):
                ns = min(128, nq - s0)
                i2 = itc[0] % 2
                itc[0] += 1
                s = "%d" % i2
                C.ld(x1t[i2][0:ns, :], X1[job][col0 + s0:col0 + s0 + ns, :], ["x1t" + s])
                for half in range(2):
                    for p in range(NFC):
                        C.mm(pf[half][0:ns, :], G[:, p, s0:s0 + ns], wdn[:, p, half * 512:(half + 1) * 512], p == 0, p == NFC - 1, ["G", "wdn"], ["pf%d" % half], inc=(p == NFC - 1))
                    C.act(junk[0:ns, 0:512], pf[half][0:ns, :], AF.Square, ["pf%d" % half], ["junk", "stat" + s], accum=stat[i2][0:ns, 4 + half:5 + half])
                    C.tt(mo[i2][0:ns, half * 512:(half + 1) * 512], pf[half][0:ns, :], gt2[job][0:ns, half * 512:(half + 1) * 512], ALU.mult, ["pf%d" % half, "gt2"], ["mo" + s])
                C.tt(stat[i2][0:ns, 6:7], stat[i2][0:ns, 4:5], stat[i2][0:ns, 5:6], ALU.add, ["stat" + s], ["stat" + s])
                C.rstd(stat[i2], ns, 6, D, "stat" + s)
                C.stt(x1t[i2][0:ns, :], mo[i2][0:ns, :], stat[i2][0:ns, 7:8], x1t[i2][0:ns, :], ALU.mult, ALU.add, ["mo" + s, "stat" + s, "x1t" + s], ["x1t" + s])
                orow = own_outrow(col0 + s0) if job == 0 else s0
                C.store(y[job][orow:orow + ns, :], x1t[i2][0:ns, :], ["x1t" + s])
            if last_of_job:
                for t in range(2):
                    C.store(fm(o_conv[job][t]), UC[:, :, t], UCALL, slow=True)
    C.finish()
    return nc


_NC = None


def _rope_cs(pos):
    inv = (10000.0 ** (-np.arange(0, ROPE, 2, dtype=np.float32) / ROPE)).astype(np.float32)
    ang = pos.astype(np.float32)[:, None] * inv[None, :]
    return np.cos(ang).astype(np.float32), np.sin(ang).astype(np.float32)


def _tables(pos):
    c, s = _rope_cs(pos)
    tm = np.ascontiguousarray(np.concatenate([c, s], axis=1))
    return tm, np.ascontiguousarray(np.concatenate([c, c], axis=1).T), np.ascontiguousarray(np.concatenate([s, s], axis=1).T)


def kernel(x_prompt, x_sample, cache_mla_ckv, cache_mla_krope, cache_sb_k, cache_sb_v, state_ffn_conv,
           c_prompt, c_sample, w_ada, b_ada, g_pre_mix, g_post_mix, g_pre_ffn, g_post_ffn,
           w_in, g_q_lat, w_uq, g_kv_lat, w_uk, w_uv, w_proj_a, w_proj_b, w_out,
           w_up, conv_w, conv_b, w_down):
    global _NC
    f = lambda a: np.ascontiguousarray(np.asarray(a, dtype=np.float32))
    bf = ml_dtypes.bfloat16
    x_prompt, x_sample = f(x_prompt), f(x_sample)
    B, T = x_prompt.shape[0], x_prompt.shape[1]
    if _NC is None:
        _NC = build_nc()
    ii = np.arange(128)
    consts = {
        "ident": np.eye(128, dtype=np.float32).astype(bf),
        "mtri": (ii[:, None] < ii[None, :]).astype(np.float32).astype(bf),
        "negu": (-(ii[:, None] >= ii[None, :]).astype(np.float32)).astype(bf),
        "ones": np.ones((1, 4096), np.float32).astype(bf),
    }
    shared = {
        "w_ada": f(w_ada)[0], "b_ada": f(b_ada)[0], "g_pre": f(g_pre_mix)[0], "g_post": f(g_post_mix)[0],
        "g_pre2": f(g_pre_ffn)[0], "g_post2": f(g_post_ffn)[0], "g_kv": f(g_kv_lat), "g_q": f(g_q_lat)[0],
        "w_in": f(w_in)[0], "w_uq": f(w_uq)[0], "w_uk": f(w_uk)[0], "w_uv": f(w_uv)[0],
        "w_pa": f(w_proj_a)[0], "w_pb": f(w_proj_b)[0], "w_out": f(w_out)[0],
        "w_up": f(w_up)[0], "conv_w": f(conv_w)[0], "conv_b": f(conv_b)[0], "w_dn": f(w_down)[0],
    }
    cs_s, cq_s, sq_s = _tables(PAST + np.arange(DEC))
    in_maps = []
    for c in range(8):
        b, j = c // 4, c % 4
        off = RUN_ROW0[0] - 1024 * j
        nvalid = NBUF - off
        xk = np.zeros((NBUF, D), np.float32)
        xk[off:] = x_prompt[b, 0:nvalid]
        pos = np.maximum(np.arange(NBUF) - off, 0)
        cs_p, _, _ = _tables(pos)
        own_rows = np.array([own_bufrow(c_) for c_ in range(NOWN)])
        _, cq_p, sq_p = _tables(pos[own_rows])
        kb_p = np.where(np.arange(NBUF) >= off, 0.0, NEGBIG).astype(np.float32)[None, :].astype(bf)
        m = dict(shared)
        m.update(consts)
        m.update({
            "xk": xk, "x_s": np.ascontiguousarray(x_sample[c]),
            "c2": np.ascontiguousarray(np.stack([f(c_prompt)[b], f(c_sample)[c]])),
            "cs_p": cs_p, "cs_s": cs_s, "cq_p": cq_p, "cq_s": cq_s, "sq_p": sq_p, "sq_s": sq_s,
            "kb_p": kb_p, "kb_s": np.zeros((1, SLEN), np.float32).astype(bf),
            "hv": np.array([[1.0 if j > 0 else 0.0, 1.0]], np.float32),
            "ca_ckv": f(cache_mla_ckv)[0, c], "ca_kr": f(cache_mla_krope)[0, c],
            "ca_k": f(cache_sb_k)[0, c].reshape(PAST, HB), "ca_v": f(cache_sb_v)[0, c].reshape(PAST, HB),
            "st_conv": f(state_ffn_conv)[0, c],
        })
        in_maps.append(m)
    res = run_bass_kernel_spmd(_NC, in_maps, core_ids=list(range(8)))
    rs = res.results

    def gp(name, w):
        out = np.zeros((B, T, w), np.float32)
        for b in range(B):
            for j in range(4):
                v = rs[b * 4 + j][name]
                out[b, 1024 * j:1024 * j + 1024] = v[0:1024]
                out[b, 4096 + 1024 * j:4096 + 1024 * j + 1024] = v[1024:2048]
        return out

    def gs(name, w):
        return np.stack([rs[c][name] for c in range(8)]).reshape(8, DEC, w)

    conv_p = np.stack([rs[b * 4 + 3]["o_conv_p"] for b in range(B)])[None]
    conv_s = np.stack([rs[c]["o_conv_s"] for c in range(8)])[None]
    return (gp("y_p", D), gs("y_s", D),
            gp("o_ckv_p", KVR)[None], gp("o_kr_p", ROPE)[None],
            gp("o_k_p", HB).reshape(1, B, T, 8, 64), gp("o_v_p", HB).reshape(1, B, T, 8, 64), conv_p,
            gs("o_ckv_s", KVR)[None], gs("o_kr_s", ROPE)[None],
            gs("o_k_s", HB).reshape(1, 8, DEC, 8, 64), gs("o_v_s", HB).reshape(1, 8, DEC, 8, 64), conv_s)
```
